# Optimizing a Trainium2 kernel written in Bass

```python
import jax, jax.numpy as jnp
from jax import lax
import numpy as np

D_MODEL = 1024
BATCH = 16
SEQ = 2048
DEPTH = 4
DEC_BATCH = 16
DEC_SEQ = 4096
PAST_LEN = 128

ATTN_GROUPS = ((128, 1), (512, 4), (2048, 16))
N_GROUPS = 3
HEADS_PER_GROUP = 4
HEAD_DIM = 128
N_ATTN_HEADS = N_GROUPS * HEADS_PER_GROUP
ATTN_QKV_W = N_ATTN_HEADS * HEAD_DIM
ATTN_OUT_W = HEADS_PER_GROUP * HEAD_DIM
D_RNN = 1280
RG_BLOCKS = 10
RG_BLOCK = D_RNN // RG_BLOCKS
RG_C = 8.0
CONV_WIDTH = 4
CONV_PAD = (2, 1)
D_FF = 2816
NORM_EPS = 1e-6
MASK_VALUE = -1e30
SPLIT_POINTS = (ATTN_QKV_W, 2 * ATTN_QKV_W, 3 * ATTN_QKV_W, 3 * ATTN_QKV_W + D_RNN, 3 * ATTN_QKV_W + 2 * D_RNN, 3 * ATTN_QKV_W + 2 * D_RNN + D_MODEL)
IN_COLS = 3 * ATTN_QKV_W + 2 * D_RNN + 2 * D_MODEL

kernel_name = 'hybrid_dilated_attn_rglru_encoder'


def rmsnorm(x, g):
    xf = x.astype(jnp.float32)
    y = xf * lax.rsqrt(jnp.mean(xf * xf, axis=-1, keepdims=True) + NORM_EPS)
    return (y * g.astype(jnp.float32)).astype(x.dtype)


def swiglu(x, w_gate, w_up, w_down):
    return (jax.nn.silu(x @ w_gate) * (x @ w_up)) @ w_down


def alibi_slopes():
    h = jnp.arange(1, N_ATTN_HEADS + 1, dtype=jnp.float32)
    return (2.0 ** (-8.0 * h / N_ATTN_HEADS)).reshape(N_GROUPS, HEADS_PER_GROUP)


def dilated_window_attention(q, k, v, window, dilation, slopes):
    B, S, H, Dh = q.shape
    n_side = (window // 2) // dilation
    blk = n_side
    L = S // dilation
    nb = -(-L // blk)
    Lp = nb * blk

    def to_sub(t):
        return t.reshape(B, L, dilation, H, Dh).transpose(0, 2, 1, 3, 4)

    qs = jnp.pad(to_sub(q), ((0, 0), (0, 0), (0, Lp - L), (0, 0), (0, 0)))
    qs = qs.reshape(B, dilation, nb, blk, H, Dh)
    pad_kv = ((0, 0), (0, 0), (blk, Lp - L + blk), (0, 0), (0, 0))

    def banded(t):
        tb = jnp.pad(to_sub(t), pad_kv).reshape(B, dilation, nb + 2, blk, H, Dh)
        return jnp.concatenate([tb[:, :, 0:nb], tb[:, :, 1:nb + 1], tb[:, :, 2:nb + 2]], axis=3)

    kb, vb = banded(k), banded(v)
    s = jnp.einsum('brnqhd,brnkhd->brhnqk', qs, kb, preferred_element_type=jnp.float32) * (Dh ** -0.5)
    rel = jnp.arange(3 * blk)[None, :] - blk - jnp.arange(blk)[:, None]
    key_pos = jnp.arange(nb)[:, None] * blk + jnp.arange(3 * blk)[None, :] - blk
    valid = (jnp.abs(rel) <= n_side)[None, :, :] & ((key_pos >= 0) & (key_pos < L))[:, None, :]
    alibi = -slopes.astype(jnp.float32)[:, None, None] * (dilation * jnp.abs(rel)).astype(jnp.float32)
    s = jnp.where(valid, s + alibi[:, None], MASK_VALUE)
    m = jnp.max(s, axis=-1, keepdims=True)
    p = jnp.exp(s - m)
    den = jnp.sum(p, axis=-1, keepdims=True)
    o = jnp.einsum('brhnqk,brnkhd->brhnqd', p.astype(vb.dtype), vb, preferred_element_type=jnp.float32) / den
    lse = (m + jnp.log(den))[..., 0]
    o = o.reshape(B, dilation, H, Lp, Dh)[:, :, :, :L].transpose(0, 3, 1, 2, 4).reshape(B, S, H, Dh)
    lse = lse.reshape(B, dilation, H, Lp)[..., :L].transpose(0, 3, 1, 2).reshape(B, S, H)
    return o, lse


def rglru_direction(x, w_a, b_a, w_x, b_x, lam, reverse):
    B, S, _ = x.shape
    xb = x.reshape(B, S, RG_BLOCKS, RG_BLOCK)
    r = jax.nn.sigmoid((jnp.einsum('bsnc,ncd->bsnd', xb, w_a).reshape(B, S, D_RNN) + b_a).astype(jnp.float32))
    i = jax.nn.sigmoid((jnp.einsum('bsnc,ncd->bsnd', xb, w_x).reshape(B, S, D_RNN) + b_x).astype(jnp.float32))
    log_a = -RG_C * r * jax.nn.softplus(-lam.astype(jnp.float32))
    a = jnp.exp(log_a)
    u = jnp.sqrt(-jnp.expm1(2.0 * log_a)) * (i * x.astype(jnp.float32))

    def step(h, au):
        a_t, u_t = au
        h = a_t * h + u_t
        return h, h

    h0 = jnp.zeros((B, D_RNN), jnp.float32)
    _, hs = lax.scan(step, h0, (a.transpose(1, 0, 2), u.transpose(1, 0, 2)), reverse=reverse)
    return hs.transpose(1, 0, 2)


def hybrid_mixer(xn, w_in, b_in, conv_w, conv_b, rg_w_a, rg_b_a, rg_w_x, rg_b_x, rg_lambda, w_proj_a, w_proj_b, w_out):
    B, S, _ = xn.shape
    z = xn @ w_in + b_in
    q, k, v, xr, gr, ga, gb = jnp.split(z, SPLIT_POINTS, axis=-1)

    q = q.reshape(B, S, N_GROUPS, HEADS_PER_GROUP, HEAD_DIM)
    k = k.reshape(B, S, N_GROUPS, HEADS_PER_GROUP, HEAD_DIM)
    v = v.reshape(B, S, N_GROUPS, HEADS_PER_GROUP, HEAD_DIM)
    slopes = alibi_slopes()
    outs, lses = [], []
    for g, (window, dilation) in enumerate(ATTN_GROUPS):
        o_g, lse_g = dilated_window_attention(q[:, :, g], k[:, :, g], v[:, :, g], window, dilation, slopes[g])
        outs.append(o_g)
        lses.append(lse_g)
    wts = jax.nn.softmax(jnp.stack(lses, axis=0), axis=0)
    o_att = jnp.sum(wts[..., None] * jnp.stack(outs, axis=0), axis=0)
    y_a = o_att.reshape(B, S, ATTN_OUT_W).astype(xn.dtype)

    xc = lax.conv_general_dilated(xr, conv_w[:, None, :], window_strides=(1,), padding=[CONV_PAD],
                                  dimension_numbers=('NWC', 'WIO', 'NWC'), feature_group_count=D_RNN) + conv_b
    h = (rglru_direction(xc, rg_w_a[0], rg_b_a[0], rg_w_x[0], rg_b_x[0], rg_lambda[0], False)
         + rglru_direction(xc, rg_w_a[1], rg_b_a[1], rg_w_x[1], rg_b_x[1], rg_lambda[1], True))
    y_b = (h * jax.nn.gelu(gr.astype(jnp.float32))).astype(xn.dtype)

    merged = jax.nn.sigmoid(ga) * (y_a @ w_proj_a) + jax.nn.sigmoid(gb) * (y_b @ w_proj_b)
    return merged @ w_out


def encoder_trunk(x, params):
    (ffn1_norm, ffn1_w_gate, ffn1_w_up, ffn1_w_down, mix_norm, w_in, b_in, conv_w, conv_b,
     rg_w_a, rg_b_a, rg_w_x, rg_b_x, rg_lambda, w_proj_a, w_proj_b, w_out,
     ffn2_norm, ffn2_w_gate, ffn2_w_up, ffn2_w_down, final_norm) = params
    for l in range(DEPTH):
        x = x + 0.5 * swiglu(rmsnorm(x, ffn1_norm[l]), ffn1_w_gate[l], ffn1_w_up[l], ffn1_w_down[l])
        x = x + hybrid_mixer(rmsnorm(x, mix_norm[l]), w_in[l], b_in[l], conv_w[l], conv_b[l],
                             rg_w_a[l], rg_b_a[l], rg_w_x[l], rg_b_x[l], rg_lambda[l],
                             w_proj_a[l], w_proj_b[l], w_out[l])
        x = x + 0.5 * swiglu(rmsnorm(x, ffn2_norm[l]), ffn2_w_gate[l], ffn2_w_up[l], ffn2_w_down[l])
    return rmsnorm(x, final_norm)


def _normal(key, shape, scale):
    return jax.random.normal(key, shape, jnp.float32) * scale


def setup_inputs(seed: int = 0) -> dict:
    key = jax.random.key(seed)
    ks = jax.random.split(key, 24)
    u = jax.random.uniform(ks[15], (DEPTH, 2, D_RNN), jnp.float32, 0.9, 0.999)
    a0 = u ** (1.0 / RG_C)
    return {
        'x_prompt': _normal(ks[0], (BATCH, SEQ, D_MODEL), 1.0),
        'x_sample': _normal(ks[1], (DEC_BATCH, DEC_SEQ, D_MODEL), 1.0),
        'ffn1_norm': 1.0 + _normal(ks[2], (DEPTH, D_MODEL), 0.02),
        'ffn1_w_gate': _normal(ks[3], (DEPTH, D_MODEL, D_FF), D_MODEL ** -0.5),
        'ffn1_w_up': _normal(ks[4], (DEPTH, D_MODEL, D_FF), D_MODEL ** -0.5),
        'ffn1_w_down': _normal(ks[5], (DEPTH, D_FF, D_MODEL), D_FF ** -0.5),
        'mix_norm': 1.0 + _normal(ks[6], (DEPTH, D_MODEL), 0.02),
        'w_in': _normal(ks[7], (DEPTH, D_MODEL, IN_COLS), D_MODEL ** -0.5),
        'b_in': _normal(ks[8], (DEPTH, IN_COLS), 0.02),
        'conv_w': _normal(ks[9], (DEPTH, CONV_WIDTH, D_RNN), CONV_WIDTH ** -0.5),
        'conv_b': _normal(ks[10], (DEPTH, D_RNN), 0.02),
        'rg_w_a': _normal(ks[11], (DEPTH, 2, RG_BLOCKS, RG_BLOCK, RG_BLOCK), RG_BLOCK ** -0.5),
        'rg_b_a': _normal(ks[12], (DEPTH, 2, D_RNN), 0.02),
        'rg_w_x': _normal(ks[13], (DEPTH, 2, RG_BLOCKS, RG_BLOCK, RG_BLOCK), RG_BLOCK ** -0.5),
        'rg_b_x': _normal(ks[14], (DEPTH, 2, D_RNN), 0.02),
        'rg_lambda': jnp.log(a0) - jnp.log1p(-a0),
        'w_proj_a': _normal(ks[16], (DEPTH, ATTN_OUT_W, D_MODEL), ATTN_OUT_W ** -0.5),
        'w_proj_b': _normal(ks[17], (DEPTH, D_RNN, D_MODEL), D_RNN ** -0.5),
        'w_out': _normal(ks[18], (DEPTH, D_MODEL, D_MODEL), D_MODEL ** -0.5),
        'ffn2_norm': 1.0 + _normal(ks[19], (DEPTH, D_MODEL), 0.02),
        'ffn2_w_gate': _normal(ks[20], (DEPTH, D_MODEL, D_FF), D_MODEL ** -0.5),
        'ffn2_w_up': _normal(ks[21], (DEPTH, D_MODEL, D_FF), D_MODEL ** -0.5),
        'ffn2_w_down': _normal(ks[22], (DEPTH, D_FF, D_MODEL), D_FF ** -0.5),
        'final_norm': 1.0 + _normal(ks[23], (D_MODEL,), 0.02),
    }


def reference(x_prompt, x_sample, ffn1_norm, ffn1_w_gate, ffn1_w_up, ffn1_w_down, mix_norm, w_in, b_in,
              conv_w, conv_b, rg_w_a, rg_b_a, rg_w_x, rg_b_x, rg_lambda, w_proj_a, w_proj_b, w_out,
              ffn2_norm, ffn2_w_gate, ffn2_w_up, ffn2_w_down, final_norm):
    params = (ffn1_norm, ffn1_w_gate, ffn1_w_up, ffn1_w_down, mix_norm, w_in, b_in, conv_w, conv_b,
              rg_w_a, rg_b_a, rg_w_x, rg_b_x, rg_lambda, w_proj_a, w_proj_b, w_out,
              ffn2_norm, ffn2_w_gate, ffn2_w_up, ffn2_w_down, final_norm)
    y_prompt = encoder_trunk(x_prompt, params)
    y_sample = encoder_trunk(x_sample, params)
    return (y_prompt, y_sample)
```

```python
import numpy as np
from contextlib import ExitStack
import concourse.bass as bass
import concourse.mybir as mybir
from concourse.bass_utils import run_bass_kernel_spmd

F32 = mybir.dt.float32
BF16 = mybir.dt.bfloat16
I32 = mybir.dt.int32
AF = mybir.ActivationFunctionType
ALU = mybir.AluOpType

D = 1024
DFF = 2816
NFF = 22
KC = 8
INC = 9216
DEPTH = 4
NCORES = 8
GROUPS = ((128, 1), (512, 4), (2048, 16))
SLOPES = [2.0 ** (-8.0 * h / 12.0) for h in range(1, 13)]
EPS = 1e-6
GELU_C = 0.7978845608028654
TG = 512


def pp_layout(L):
    off = {}
    n = 0
    for name, cnt in (("n1", L * 8), ("nm", L * 8), ("n2", L * 8), ("nf", 8), ("bin", L * 72),
                      ("cw", L * 40), ("cb", L * 10), ("ba", L * 20), ("bx", L * 20), ("lam", L * 20)):
        off[name] = n
        n += cnt
    return off, n


class Buf:
    __slots__ = ("w", "r")

    def __init__(self):
        self.w = None
        self.r = {}


class TB:
    __slots__ = ("t", "b")

    def __init__(self, t, b=None):
        self.t = t
        self.b = b if b is not None else Buf()


class Eng:
    def __init__(self, name, h, key):
        self.name = name
        self.h = h
        self.key = key
        self.cnt = 0
        self.waited = {}
        self.dkeys = []
        self.dma_n = 0


class Sched:
    def __init__(self, nc):
        self.nc = nc
        self.sems = {}
        self.E = {}
        self.banks = []
        self.bank_i = 0

    def add_engine(self, name, h, sem, dsems=()):
        key = "e_" + name
        self.sems[key] = sem
        e = Eng(name, h, key)
        for i, s in enumerate(dsems):
            k = "d_%s_%d" % (name, i)
            self.sems[k] = s
            e.dkeys.append(k)
        self.E[name] = e

    def _need(self, e, tok):
        key, val = tok
        if e.waited.get(key, 0) >= val:
            return
        if key == e.key and e.name == "pe":
            return
        e.h.wait_ge(self.sems[key], val)
        e.waited[key] = val

    def _deps(self, e, reads, writes):
        for b in reads:
            if b.w is not None:
                self._need(e, b.w)
        for b in writes:
            if b.w is not None:
                self._need(e, b.w)
            for k, v in b.r.items():
                self._need(e, (k, v))

    def _commit(self, tok, reads, writes):
        k, v = tok
        for b in reads:
            if b.r.get(k, 0) < v:
                b.r[k] = v
        for b in writes:
            b.w = tok
            b.r = {}

    def op(self, en, fn, reads=(), writes=()):
        e = self.E[en]
        self._deps(e, reads, writes)
        ins = fn(e.h)
        e.cnt += 1
        ins.then_inc(self.sems[e.key], 1)
        tok = (e.key, e.cnt)
        self._commit(tok, reads, writes)
        return tok

    def mm(self, fns, reads=(), writes=()):
        e = self.E["pe"]
        self._deps(e, reads, writes)
        ins = None
        for f in fns:
            ins = f(e.h)
        e.cnt += 1
        ins.then_inc(self.sems[e.key], 1)
        tok = (e.key, e.cnt)
        self._commit(tok, reads, writes)
        return tok

    def dma(self, qn, out, in_, reads=(), writes=()):
        q = self.E[qn]
        self._deps(q, reads, writes)
        K = len(q.dkeys)
        k = q.dma_n % K
        rnd = q.dma_n // K
        key = q.dkeys[k]
        if rnd > 0:
            self._need(q, (key, 16 * rnd))
        ins = q.h.dma_start(out=out, in_=in_)
        ins.then_inc(self.sems[key], 16)
        q.dma_n += 1
        tok = (key, 16 * (rnd + 1))
        self._commit(tok, reads, writes)
        return tok

    def barrier(self):
        toks = []
        for e in self.E.values():
            if e.cnt > 0:
                toks.append((e.key, e.cnt))
            K = len(e.dkeys)
            for k in range(K):
                n = (e.dma_n - k + K - 1) // K
                if n > 0:
                    toks.append((e.dkeys[k], 16 * n))
        for e in self.E.values():
            for t in toks:
                self._need(e, t)

    def bank(self):
        b = self.banks[self.bank_i % len(self.banks)]
        self.bank_i += 1
        return b


def MM(out, lhsT, rhs, start, stop):
    return lambda h: h.matmul(out, lhsT, rhs, start=start, stop=stop)


def TR(out, in_, ident):
    return lambda h: h.transpose(out, in_, ident)


def ACT(out, in_, func, bias=None, scale=None):
    kw = {}
    if bias is not None:
        kw["bias"] = bias
    if scale is not None:
        kw["scale"] = scale
    return lambda h: h.activation(out=out, in_=in_, func=func, **kw)


def STT(out, in0, scalar, in1, op0, op1):
    return lambda h: h.scalar_tensor_tensor(out=out, in0=in0, scalar=scalar, in1=in1, op0=op0, op1=op1)


def TT(out, in0, in1, op):
    return lambda h: h.tensor_tensor(out=out, in0=in0, in1=in1, op=op)


def TS(out, in0, s1, s2, op0, op1=None):
    if op1 is None:
        return lambda h: h.tensor_scalar(out=out, in0=in0, scalar1=s1, scalar2=None, op0=op0)
    return lambda h: h.tensor_scalar(out=out, in0=in0, scalar1=s1, scalar2=s2, op0=op0, op1=op1)


def CP(out, in_):
    return lambda h: h.tensor_copy(out=out, in_=in_)


def build(seq_lens, L, dbg=None):
    NTOK = sum(seq_lens)
    SMAX = max(seq_lens)
    PO, NPP = pp_layout(L)
    nc = bass.Bass("TRN2", target_bir_lowering=False)
    x_d = nc.dram_tensor("x", [NTOK, D], F32, kind="ExternalInput").ap()
    y_d = nc.dram_tensor("y", [NTOK, D], F32, kind="ExternalOutput").ap()
    wshape = {
        "wgu": [L, 2, NFF, 128, 2 * KC * 128],
        "wd": [L, 2, 8, 2, 128, 11 * 128],
        "win": [L, 72, 128, KC * 128],
        "wrg": [L, 10, 128, 4 * 128],
        "wpa": [L, 8, 128, 4 * 128],
        "wpb": [L, 8, 128, 10 * 128],
        "wo": [L, 8, 128, 8 * 128],
    }
    wf = {k: nc.dram_tensor(k, s, F32, kind="ExternalInput").ap() for k, s in wshape.items()}
    wb = {k: nc.dram_tensor(k + "_b", s, BF16, kind="Internal").ap() for k, s in wshape.items()}
    pp_d = nc.dram_tensor("pp", [128, NPP], F32, kind="ExternalInput").ap()
    bin_d = nc.dram_tensor("b_in", [L, INC], F32, kind="ExternalInput").ap()
    xres_d = nc.dram_tensor("xres", [128, 8, SMAX], F32, kind="Internal").ap()
    ya_d = nc.dram_tensor("ya_s", [128, 4, SMAX], BF16, kind="Internal").ap()
    yb_d = nc.dram_tensor("yb_s", [128, 10, SMAX], BF16, kind="Internal").ap()
    e_d = nc.dram_tensor("e_s", [128, 12, 256], BF16, kind="Internal").ap()
    dbg_d = {}
    if dbg:
        for name, shp in dbg.items():
            dbg_d[name] = nc.dram_tensor("dbg_" + name, shp, F32, kind="ExternalOutput").ap()

    uid = [0]

    def sb(stack, name, shape, dt):
        uid[0] += 1
        return stack.enter_context(nc.sbuf_tensor("%s_%d" % (name, uid[0]), shape, dt))

    with ExitStack() as top:
        S = Sched(nc)
        nsem = [0]

        def newsem():
            nsem[0] += 1
            return top.enter_context(nc.semaphore("s%d" % nsem[0]))

        S.add_engine("pe", nc.tensor, newsem())
        S.add_engine("act", nc.scalar, newsem())
        S.add_engine("dve", nc.vector, newsem())
        S.add_engine("pool", nc.gpsimd, newsem(), [newsem() for _ in range(12)])
        S.add_engine("sp", nc.sync, newsem(), [newsem() for _ in range(12)])
        for i in range(7):
            S.banks.append(TB(top.enter_context(nc.psum_tensor("ps%d" % i, [128, 512], F32))))
        statbank = TB(top.enter_context(nc.psum_tensor("ps_stat", [128, 512], F32)))

        ppt = TB(sb(top, "pp", [128, NPP], F32))
        dpt = TB(sb(top, "dp", [128, NPP], F32))
        identf = TB(sb(top, "identf", [128, 128], F32))
        onesb = TB(sb(top, "onesb", [128, 128], BF16))
        eB = Buf()
        xn2 = sb(top, "xn2", [128, KC, SMAX], BF16)
        WBN = 6
        wbufs = []
        wbi = [0]

        def alloc_wbufs(stk, ncols):
            wbufs[:] = [TB(sb(stk, "wbuf%d" % i, [128, ncols], BF16)) for i in range(WBN)]

        def ppc(name, idx):
            return ppt.t[:, PO[name] + idx: PO[name] + idx + 1]

        def dpc(name, idx):
            return dpt.t[:, PO[name] + idx: PO[name] + idx + 1]

        S.dma("sp", ppt.t[:], pp_d[:, :], writes=[ppt.b])

        wB = {}

        def cast_jobs(l):
            jobs = []
            for f in range(2):
                for m in range(NFF):
                    jobs.append((wb["wgu"][l, f, m], wf["wgu"][l, f, m], [("wgu", l, f, m)]))
                for mo in range(8):
                    jobs.append((wb["wd"][l, f, mo], wf["wd"][l, f, mo], [("wd", l, f, mo)]))
            for m4 in range(18):
                jobs.append((wb["win"][l, 4 * m4:4 * m4 + 4], wf["win"][l, 4 * m4:4 * m4 + 4], [("win", l, 4 * m4 + j) for j in range(4)]))
            for key, nsplit in (("wrg", 1), ("wpa", 1), ("wpb", 2), ("wo", 2)):
                n = wshape[key][1]
                step = n // nsplit
                for s0 in range(0, n, step):
                    jobs.append((wb[key][l, s0:s0 + step], wf[key][l, s0:s0 + step], [(key, l, j) for j in range(s0, s0 + step)]))
            return jobs

        def issue_casts(jobs):
            for (o, i, keys) in jobs:
                b = Buf()
                S.dma("pool", o, i, writes=[b])
                for kk in keys:
                    wB[kk] = b

        def cast_layer(l):
            issue_casts(cast_jobs(l))

        cast_layer(0)

        def load_w(src_ap, ncols, bkey):
            w = wbufs[wbi[0] % WBN]
            wbi[0] += 1
            S.dma("sp", w.t[:, 0:ncols], src_ap, reads=[wB[bkey]], writes=[w.b])
            return w

        with ExitStack() as st0:
            idi = TB(sb(st0, "idi", [128, 256], I32))
            relf = TB(sb(st0, "relf", [128, 256], F32))
            absr = TB(sb(st0, "absr", [128, 256], F32))
            msk = TB(sb(st0, "msk", [128, 256], F32))
            tmpE = TB(sb(st0, "tmpE", [128, 256], F32))
            Et = TB(sb(st0, "E", [128, 12, 256], BF16))
            S.op("pool", lambda h: h.iota(idi.t[:, 0:128], pattern=[[-1, 128]], base=0, channel_multiplier=1),
                 writes=[idi.b])
            S.op("dve", CP(relf.t[:, 0:128], idi.t[:, 0:128]), reads=[idi.b], writes=[relf.b])
            S.op("dve", TS(identf.t[:], relf.t[:, 0:128], 0.0, None, ALU.is_equal), reads=[relf.b], writes=[identf.b])
            S.op("dve", lambda h: h.memset(onesb.t[:], 1.0), writes=[onesb.b])
            S.op("pool", lambda h: h.iota(idi.t[:], pattern=[[-1, 256]], base=64, channel_multiplier=1),
                 reads=[relf.b], writes=[idi.b])
            S.op("dve", CP(relf.t[:], idi.t[:]), reads=[idi.b], writes=[relf.b])
            S.op("act", ACT(absr.t[:], relf.t[:], AF.Abs), reads=[relf.b], writes=[absr.b])
            S.op("dve", TS(msk.t[:], absr.t[:], 64.5, None, ALU.is_le), reads=[absr.b], writes=[msk.b])
            for hh in range(12):
                d = GROUPS[hh // 4][1]
                S.op("act", ACT(tmpE.t[:], absr.t[:], AF.Exp, scale=-SLOPES[hh] * d), reads=[absr.b], writes=[tmpE.b])
                S.op("dve", TT(Et.t[:, hh, :], tmpE.t[:], msk.t[:], ALU.mult), reads=[tmpE.b, msk.b], writes=[Et.b])
            S.dma("sp", e_d[:, :, :], Et.t[:], reads=[Et.b], writes=[eB])
            S.op("dve", CP(dpt.t[:], ppt.t[:]), reads=[ppt.b], writes=[dpt.b])
            for name, cnt in (("bin", L * 72), ("ba", L * 20), ("bx", L * 20)):
                o = PO[name]
                S.op("dve", TS(dpt.t[:, o:o + cnt], ppt.t[:, o:o + cnt], 0.5, None, ALU.mult), reads=[ppt.b], writes=[dpt.b])
            o = PO["lam"]
            cnt = L * 20
            lt = TB(sb(st0, "lt", [128, cnt], F32))
            S.op("act", ACT(lt.t[:], ppt.t[:, o:o + cnt], AF.Exp, scale=-1.0), reads=[ppt.b], writes=[lt.b])
            S.op("act", ACT(lt.t[:], lt.t[:], AF.Ln, bias=1.0), reads=[lt.b], writes=[lt.b])
            S.op("dve", TS(dpt.t[:, o:o + cnt], lt.t[:], -4.0, None, ALU.mult), reads=[lt.b], writes=[dpt.b])
            S.barrier()

        def norm_sq(xg, xgB, kc, cols, st):
            sq = st["sq"][st["sqi"] % 3]
            st["sqi"] += 1
            S.op("act", ACT(sq.t[:], xg[:, kc, cols], AF.Square), reads=[xgB[kc]], writes=[sq.b])
            return (sq, kc)

        def norm_mm(sq, kc):
            S.mm([MM(statbank.t[:], onesb.t[:], sq.t[:], kc == 0, kc == KC - 1)], reads=[sq.b, onesb.b], writes=[statbank.b])

        def norm_stat(xg, xgB, kc, cols, st):
            norm_mm(*norm_sq(xg, xgB, kc, cols, st))

        def norm_fin(xg, xgB, cols, gname, gidx0, out_fn, outB, st):
            sd = st["sd"]
            S.op("act", ACT(sd.t[:], statbank.t[:], AF.Sqrt, bias=st["epsb"].t[:, 0:1], scale=1.0 / D), reads=[statbank.b, st["epsb"].b], writes=[sd.b])
            S.op("dve", lambda h: h.reciprocal(out=sd.t[:], in_=sd.t[:]), reads=[sd.b], writes=[sd.b])
            for kc in range(KC):
                S.op("dve", STT(out_fn(kc), xg[:, kc, cols], ppc(gname, gidx0 + kc), sd.t[:], ALU.mult, ALU.mult),
                     reads=[xgB[kc], sd.b, ppt.b], writes=[outB(kc) if callable(outB) else outB])

        def norm_tile(xg, xgB, cols, gname, gidx0, out_fn, outB, st, stats_done=False):
            if not stats_done:
                for kc in range(KC):
                    norm_stat(xg, xgB, kc, cols, st)
            norm_fin(xg, xgB, cols, gname, gidx0, out_fn, outB, st)

        def ffn(l, f, xg, xgB, G, st, stats_done=False):
            nt = G // TG
            assert nt == 1
            xn = st["xn"]
            for t in range(nt):
                cols = slice(t * TG, (t + 1) * TG)
                norm_tile(xg, xgB, cols, "n1" if f == 0 else "n2", l * 8, lambda kc: xn.t[:, kc, cols], lambda kc: st["xnB"][kc], st, stats_done)
            h = st["h"]
            for half in range(2):
                for mi in range(11):
                    m = half * 11 + mi
                    w = load_w(wb["wgu"][l, f, m], 2048, ("wgu", l, f, m))
                    for t in range(nt):
                        cols = slice(t * TG, (t + 1) * TG)
                        bg = S.bank()
                        bu = S.bank()
                        if m == 0:
                            for kc in range(KC):
                                S.mm([MM(bg.t[:], w.t[:, kc * 128:(kc + 1) * 128], xn.t[:, kc, cols], kc == 0, kc == KC - 1)],
                                     reads=[w.b, st["xnB"][kc]], writes=[bg.b])
                                S.mm([MM(bu.t[:], w.t[:, (8 + kc) * 128:(9 + kc) * 128], xn.t[:, kc, cols], kc == 0, kc == KC - 1)],
                                     reads=[w.b, st["xnB"][kc]], writes=[bu.b])
                        else:
                            S.mm([MM(bg.t[:], w.t[:, kc * 128:(kc + 1) * 128], xn.t[:, kc, cols], kc == 0, kc == KC - 1) for kc in range(KC)],
                                 reads=[w.b] + st["xnB"], writes=[bg.b])
                            S.mm([MM(bu.t[:], w.t[:, (8 + kc) * 128:(9 + kc) * 128], xn.t[:, kc, cols], kc == 0, kc == KC - 1) for kc in range(KC)],
                                 reads=[w.b] + st["xnB"], writes=[bu.b])
                        T = st["T"][st["Ti"] % 2]
                        W = st["W"][st["Ti"] % 2]
                        st["Ti"] += 1
                        S.op("act", ACT(T.t[:], bg.t[:], AF.Tanh, scale=0.5), reads=[bg.b], writes=[T.b])
                        S.op("dve", STT(W.t[:], T.t[:], 1.0, bg.t[:], ALU.add, ALU.mult), reads=[T.b, bg.b], writes=[W.b])
                        S.op("dve", TT(h.t[:, mi, cols], W.t[:], bu.t[:], ALU.mult), reads=[W.b, bu.b], writes=[st["hB"][mi]])
                pend = None
                for mo in range(8):
                    w = load_w(wb["wd"][l, f, mo, half], 1408, ("wd", l, f, mo))
                    for t in range(nt):
                        cols = slice(t * TG, (t + 1) * TG)
                        by = S.bank()
                        S.mm([MM(by.t[:], w.t[:, kc * 128:(kc + 1) * 128], h.t[:, kc, cols], kc == 0, kc == 10) for kc in range(11)],
                             reads=[w.b] + st["hB"], writes=[by.b])
                        S.op("dve", STT(xg[:, mo, cols], by.t[:], 0.25, xg[:, mo, cols], ALU.mult, ALU.add),
                             reads=[by.b, xgB[mo]], writes=[xgB[mo]])
                        if half == 1:
                            nsq = norm_sq(xg, xgB, mo, cols, st)
                            if pend is not None:
                                norm_mm(*pend)
                            pend = nsq
                if pend is not None:
                    norm_mm(*pend)

        for si, SL in enumerate(seq_lens):
            tok0 = sum(seq_lens[:si])
            G = TG
            NG = SL // G
            xn2B = [Buf() for _ in range(NG)]
            xresB = [Buf() for _ in range(NG)]
            yaB = [Buf() for _ in range(4)]
            ybB = [Buf() for _ in range(10)]

            def alloc_ac(stk):
                st = {}
                st["xgs"] = [sb(stk, "xg", [128, KC, G], F32) for _ in range(2)]
                st["xgBs"] = [[Buf() for _ in range(KC)] for _ in range(2)]
                st["xn"] = TB(sb(stk, "xn", [128, KC, G], BF16))
                st["xnB"] = [Buf() for _ in range(KC)]
                st["h"] = TB(sb(stk, "h", [128, 11, G], BF16))
                st["hB"] = [Buf() for _ in range(11)]
                st["sq"] = [TB(sb(stk, "sq", [128, TG], BF16)) for _ in range(3)]
                st["sqi"] = 0
                st["sd"] = TB(sb(stk, "sd", [128, TG], F32))
                st["T"] = [TB(sb(stk, "T", [128, TG], F32)) for _ in range(2)]
                st["W"] = [TB(sb(stk, "W", [128, TG], F32)) for _ in range(2)]
                st["Ti"] = 0
                st["epsb"] = TB(sb(stk, "epsb", [128, 1], F32))
                S.op("dve", lambda h: h.memset(st["epsb"].t[:], EPS), writes=[st["epsb"].b])
                one_io = TB(sb(stk, "io", [128, D], F32))
                st["io"] = [one_io, one_io]
                st["ioi"] = 0
                return st

            def phase_a(l, gi, st, stats_done=False, store_q="pool"):
                xg, xgB = st["xgs"][gi % 2], st["xgBs"][gi % 2]
                ffn(l, 0, xg, xgB, G, st, stats_done)
                for t in range(G // TG):
                    cols = slice(t * TG, (t + 1) * TG)
                    gcols = slice(gi * G + t * TG, gi * G + (t + 1) * TG)
                    norm_tile(xg, xgB, cols, "nm", l * 8, lambda kc: xn2[:, kc, gcols], xn2B[gi], st, True)
                S.dma(store_q, xres_d[:, :, gi * G:(gi + 1) * G], xg[:], reads=xgB, writes=[xresB[gi]])

            with ExitStack() as stk:
                alloc_wbufs(stk, 2048)
                st = alloc_ac(stk)
                for gi in range(NG):
                    xg, xgB = st["xgs"][gi % 2], st["xgBs"][gi % 2]
                    for tt in range(G // 128):
                        io = st["io"][st["ioi"] % 2]
                        st["ioi"] += 1
                        r0 = tok0 + gi * G + tt * 128
                        S.dma("sp", io.t[:], x_d[r0:r0 + 128, :], writes=[io.b])
                        for half in range(2):
                            bank = S.bank()
                            S.mm([TR(bank.t[:, j * 128:(j + 1) * 128], io.t[:, (half * 4 + j) * 128:(half * 4 + j + 1) * 128], identf.t[:])
                                  for j in range(4)], reads=[io.b, identf.b], writes=[bank.b])
                            for j in range(4):
                                kc = half * 4 + j
                                S.op("act", ACT(xg[:, kc, tt * 128:(tt + 1) * 128], bank.t[:, j * 128:(j + 1) * 128], AF.Copy),
                                     reads=[bank.b], writes=[xgB[kc]])
                    phase_a(0, gi, st, False, "sp" if si == 0 else "pool")
                S.barrier()

            for l in range(L):
                with ExitStack() as stk:
                    alloc_wbufs(stk, 1024)
                    bvbc = TB(sb(stk, "bvbc", [128, 1536], F32))
                    Et = TB(sb(stk, "E", [128, 12, 256], BF16))
                    S.dma("sp", Et.t[:], e_d[:, :, :], reads=[eB], writes=[Et.b])
                    S.dma("sp", bvbc.t[:], bin_d[l:l + 1, 3072:4608].partition_broadcast(128), writes=[bvbc.b])
                    qT = [TB(sb(stk, "qT", [128, SL], BF16)) for _ in range(2)]
                    kT = [TB(sb(stk, "kT", [128, SL], BF16)) for _ in range(2)]
                    Vt = [TB(sb(stk, "Vt", [128, SL // 128, 128], BF16)) for _ in range(2)]
                    acc = TB(sb(stk, "acc", [128, 2, SL], F32))
                    Xb = [TB(sb(stk, "Xb", [128, 256], F32)) for _ in range(4)]
                    Pb = [TB(sb(stk, "Pb", [128, 256], BF16)) for _ in range(10)]
                    accB = [Buf() for _ in range(4)]
                    fz = TB(sb(stk, "fz", [128, 2], F32))
                    yab = TB(sb(stk, "yab", [128, SL], BF16))
                    heads = [(hs, g) for hs in range(4) for g in range(3)]

                    def load_head(hs, g):
                        hh = 4 * g + hs
                        return [load_w(wb["win"][l, mm_], 1024, ("win", l, mm_)) for mm_ in (hh, 12 + hh, 24 + hh)]

                    nxt = load_head(*heads[0])
                    for hi, (hs, g) in enumerate(heads):
                        hh = 4 * g + hs
                        d = GROUPS[g][1]
                        Lr = SL // d
                        nkt = Lr // 128
                        wq, wk, wv = nxt
                        if hi + 1 < len(heads):
                            nxt = load_head(*heads[hi + 1])
                        q = qT[hi % 2]
                        k = kT[hi % 2]
                        V = Vt[hi % 2]
                        qv = q.t[:].rearrange("p (r m) -> p r m", r=d)
                        kv = k.t[:].rearrange("p (r m) -> p r m", r=d)
                        for (wt, dst, dstB, mcol) in ((wq, qv, q.b, hh), (wk, kv, k.b, 12 + hh)):
                            for ti in range(SL // TG):
                                bank = S.bank()
                                S.mm([MM(bank.t[:], wt.t[:, kc * 128:(kc + 1) * 128], xn2[:, kc, ti * TG:(ti + 1) * TG], kc == 0, kc == KC - 1)
                                      for kc in range(KC)], reads=[wt.b, xn2B[ti * TG // G]], writes=[bank.b])
                                m0 = ti * TG // d
                                S.op("act", ACT(dst[:, :, m0:m0 + TG // d], bank.t[:].rearrange("p (m r) -> p r m", r=d), AF.Identity,
                                                bias=ppc("bin", l * 72 + mcol)), reads=[bank.b, ppt.b], writes=[dstB])
                        xv = xn2[:, :, 0:SL].rearrange("p c (m r) -> p c r m", r=d)
                        ntile = d * nkt
                        for t4 in range(0, ntile, 4):
                            bank = S.bank()
                            fns = []
                            for j in range(4):
                                r, kt = divmod(t4 + j, nkt)
                                for kc in range(KC):
                                    fns.append(MM(bank.t[:, j * 128:(j + 1) * 128], xv[:, kc, r, kt * 128:(kt + 1) * 128],
                                                  wv.t[:, kc * 128:(kc + 1) * 128], kc == 0, kc == KC - 1))
                            S.mm(fns, reads=[wv.b] + xn2B, writes=[bank.b])
                            S.op("dve", TT(V.t[:, t4:t4 + 4, :], bank.t[:].rearrange("p (j c) -> p j c", j=4),
                                           bvbc.t[:, hh * 128:(hh + 1) * 128].unsqueeze(1).to_broadcast([128, 4, 128]), ALU.add),
                                 reads=[bank.b, bvbc.b], writes=[V.b])
                        accv = acc.t[:].rearrange("p u (m r) -> p u r m", r=d)
                        S.op("dve", lambda h: h.memset(fz.t[:, 0:1], 0.0), reads=accB, writes=accB + [fz.b])
                        its = [(r, kt) for r in range(d) for kt in range(nkt + 1)]
                        Pof = {}

                        def stage1(idx):
                            r, kt = its[idx]
                            if kt >= nkt:
                                return
                            qs = max(0, 128 * kt - 64)
                            qe = min(Lr, 128 * kt + 192)
                            c0 = qs - (128 * kt - 64)
                            ncol = qe - qs
                            bank = S.bank()
                            S.mm([MM(bank.t[:, 0:ncol], kv[:, r, kt * 128:(kt + 1) * 128], qv[:, r, qs:qe], True, True)],
                                 reads=[k.b, q.b], writes=[bank.b])
                            X = Xb[idx % len(Xb)]
                            P = Pb[idx % len(Pb)]
                            S.op("act", ACT(X.t[:, 0:ncol], bank.t[:, 0:ncol], AF.Exp, scale=128.0 ** -0.5), reads=[bank.b], writes=[X.b])
                            S.op("pool", TT(P.t[:, c0:c0 + ncol], X.t[:, 0:ncol], Et.t[:, hh, c0:c0 + ncol], ALU.mult),
                                 reads=[X.b, Et.b], writes=[P.b])
                            Pof[(r, kt)] = P

                        def stage2(idx):
                            r, kt = its[idx]
                            j = kt
                            a = max(0, 128 * j - 64)
                            bnd = min(Lr, 128 * j + 64)
                            n = bnd - a
                            srcs = [(Pof[(r, kk)], kk) for kk in (kt - 1, kt) if 0 <= kk < nkt]
                            bank = S.bank()
                            fns = []
                            for which in range(2):
                                for si_, (Pm, ktp) in enumerate(srcs):
                                    cc = a - (128 * ktp - 64)
                                    lhs = V.t[:, r * nkt + ktp, :] if which == 0 else onesb.t[:]
                                    fns.append(MM(bank.t[:, which * 128:which * 128 + n], lhs, Pm.t[:, cc:cc + n], si_ == 0, si_ == len(srcs) - 1))
                            S.mm(fns, reads=[V.b, onesb.b] + [s_[0].b for s_ in srcs], writes=[bank.b])
                            src = bank.t[:, 0:256].rearrange("p (u c) -> p u c", u=2)[:, :, 0:n]
                            dst = accv[:, :, r, a:bnd]
                            ab = accB[idx % 4]
                            if g == 0:
                                S.op("act", ACT(dst, src, AF.Copy), reads=[bank.b], writes=[ab])
                            else:
                                S.op("dve", TT(dst, src, dst, ALU.add), reads=[bank.b, ab], writes=[ab])

                        LEAD = 5
                        for idx in range(len(its) + LEAD):
                            if idx < len(its):
                                stage1(idx)
                            if idx >= LEAD:
                                stage2(idx - LEAD)
                        if g == 2:
                            S.op("dve", lambda h: h.reciprocal(out=acc.t[:, 1, :], in_=acc.t[:, 1, :]), reads=accB, writes=accB)
                            S.op("dve", TT(yab.t[:], acc.t[:, 0, :], acc.t[:, 1, :], ALU.mult), reads=accB, writes=[yab.b])
                            S.dma("sp", ya_d[:, hs, 0:SL], yab.t[:], reads=[yab.b], writes=[yaB[hs]])
                    S.barrier()

                with ExitStack() as stk:
                    alloc_wbufs(stk, 1024)
                    SPW = 1024
                    NSP = SL // SPW
                    PPS = SPW // TG
                    xr = TB(sb(stk, "xr", [128, SL + 4], F32))
                    xc = sb(stk, "xc", [128, SL], F32)
                    xcb = sb(stk, "xcb", [128, SL], BF16)
                    xcB = [Buf() for _ in range(NSP)]
                    xcbB = [Buf() for _ in range(NSP)]
                    hf = TB(sb(stk, "hf", [128, SL], F32))
                    Ab = [TB(sb(stk, "A", [128, SPW], F32)) for _ in range(2)]
                    Ub = [TB(sb(stk, "U", [128, SPW], F32)) for _ in range(2)]
                    Qbs = [TB(sb(stk, "Q", [128, SPW], F32)) for _ in range(2)]
                    HB = TB(sb(stk, "HB", [128, SPW], F32))
                    HS = TB(sb(stk, "HS", [128, SPW], F32))
                    hini = [TB(sb(stk, "hini", [128, 1], F32)) for _ in range(2)]
                    GXb = [TB(sb(stk, "GX", [128, SPW], F32)) for _ in range(2)]
                    Zbb = [TB(sb(stk, "Z", [128, SPW], F32)) for _ in range(2)]
                    YB = [TB(sb(stk, "YB", [128, SPW], BF16)) for _ in range(2)]
                    Trb = [TB(sb(stk, "Tr", [128, TG], F32)) for _ in range(2)]
                    Tib = [TB(sb(stk, "Ti", [128, TG], F32)) for _ in range(2)]
                    cnt = {"pc": 0, "yb": 0, "job": 0}
                    S.op("pool", lambda h: h.memset(xr.t[:], 0.0), writes=[xr.b])
                    qb = TB(sb(stk, "qb", [128, 1], F32))
                    S.op("dve", lambda h: h.memset(qb.t[:], 0.0625), writes=[qb.b])

                    def load_chunk(c):
                        return [load_w(wb["win"][l, 36 + c], 1024, ("win", l, 36 + c)),
                                load_w(wb["win"][l, 46 + c], 1024, ("win", l, 46 + c)),
                                load_w(wb["wrg"][l, c], 512, ("wrg", l, c))]

                    def xr_proj(c, wxr):
                        for ti in range(SL // TG):
                            bank = S.bank()
                            S.mm([MM(bank.t[:], wxr.t[:, kc * 128:(kc + 1) * 128], xn2[:, kc, ti * TG:(ti + 1) * TG], kc == 0, kc == KC - 1)
                                  for kc in range(KC)], reads=[wxr.b, xn2B[ti * TG // G]], writes=[bank.b])
                            S.op("act", ACT(xr.t[:, 2 + ti * TG:2 + (ti + 1) * TG], bank.t[:], AF.Identity, bias=ppc("bin", l * 72 + 36 + c)),
                                 reads=[bank.b, ppt.b], writes=[xr.b])

                    dgs = [TB(sb(stk, "dg", [128, 4, 128], F32)) for _ in range(2)]

                    def conv_diag(c):
                        dg = dgs[c % 2]
                        for jt in range(4):
                            S.op("pool", TS(dg.t[:, jt, :], identf.t[:], ppc("cw", l * 40 + jt * 10 + c), None, ALU.mult),
                                 reads=[identf.b, ppt.b], writes=[dg.b])

                    conv_banks = {}

                    def conv_mm(c, sp):
                        dg = dgs[c % 2]
                        c0 = sp * SPW
                        bl = []
                        for pp_ in range(PPS):
                            t0 = c0 + pp_ * TG
                            bank = S.bank()
                            S.mm([MM(bank.t[:], dg.t[:, jt, :], xr.t[:, t0 + jt:t0 + jt + TG], jt == 0, jt == 3) for jt in range(4)],
                                 reads=[dg.b, xr.b], writes=[bank.b])
                            bl.append(bank)
                        conv_banks[(c, sp)] = bl

                    def conv_ev(c, sp):
                        c0 = sp * SPW
                        for pp_, bank in enumerate(conv_banks.pop((c, sp))):
                            t0 = c0 + pp_ * TG
                            S.op("act", ACT(xc[:, t0:t0 + TG], bank.t[:], AF.Identity, bias=ppc("cb", l * 10 + c)), reads=[bank.b, ppt.b], writes=[xcB[sp]])
                        S.op("pool", CP(xcb[:, c0:c0 + SPW], xc[:, c0:c0 + SPW]), reads=[xcB[sp]], writes=[xcbB[sp]])

                    def conv_sp(c, sp):
                        conv_mm(c, sp)
                        conv_ev(c, sp)

                    Wc = [load_chunk(0)]
                    xr_proj(0, Wc[0][0])
                    conv_diag(0)
                    conv_sp(0, 0)
                    pend_casts = cast_jobs(l + 1) if (si == 0 and l + 1 < L) else []
                    for c in range(10):
                        wxr, wgr, wrg = Wc[c]
                        if c + 1 < 10:
                            Wc.append(load_chunk(c + 1))
                        if pend_casts and c % 2 == 0:
                            npart = (len(pend_casts) + (4 - c // 2)) // (5 - c // 2)
                            issue_casts(pend_casts[:npart])
                            pend_casts = pend_casts[npart:]

                        jobs = [(0, sp) for sp in range(NSP)] + [(1, sp) for sp in range(NSP - 1, -1, -1)]
                        jst = {}

                        def stage1a(k):
                            dr, sp = jobs[k]
                            c0 = sp * SPW
                            j = cnt["job"]
                            cnt["job"] += 1
                            A = Ab[j % 2]
                            U = Ub[j % 2]
                            Qb = Qbs[j % 2]
                            pidx = l * 20 + dr * 10 + c
                            trs = []
                            for pp_ in range(PPS):
                                cols = slice(c0 + pp_ * TG, c0 + (pp_ + 1) * TG)
                                Tr = Trb[cnt["pc"] % 2]
                                Ti = Tib[cnt["pc"] % 2]
                                cnt["pc"] += 1
                                ba_ = S.bank()
                                bx_ = S.bank()
                                S.mm([MM(ba_.t[:], wrg.t[:, (2 * dr) * 128:(2 * dr + 1) * 128], xcb[:, cols], True, True)], reads=[wrg.b, xcbB[sp]], writes=[ba_.b])
                                S.mm([MM(bx_.t[:], wrg.t[:, (2 * dr + 1) * 128:(2 * dr + 2) * 128], xcb[:, cols], True, True)], reads=[wrg.b, xcbB[sp]], writes=[bx_.b])
                                S.op("act", ACT(Tr.t[:], ba_.t[:], AF.Tanh, bias=dpc("ba", pidx), scale=0.5), reads=[ba_.b, dpt.b], writes=[Tr.b])
                                S.op("act", ACT(Ti.t[:], bx_.t[:], AF.Tanh, bias=dpc("bx", pidx), scale=0.5), reads=[bx_.b, dpt.b], writes=[Ti.b])
                                trs.append((Tr, Ti))
                            GX = Zb = None
                            if dr == 1:
                                GX = GXb[j % 2]
                                Zb = Zbb[j % 2]
                                for pp_ in range(PPS):
                                    pcols = slice(c0 + pp_ * TG, c0 + (pp_ + 1) * TG)
                                    lc = slice(pp_ * TG, (pp_ + 1) * TG)
                                    bank = S.bank()
                                    S.mm([MM(bank.t[:], wgr.t[:, kc * 128:(kc + 1) * 128], xn2[:, kc, pcols], kc == 0, kc == KC - 1) for kc in range(KC)],
                                         reads=[wgr.b, xn2B[(c0 + pp_ * TG) // G]], writes=[bank.b])
                                    S.op("act", ACT(GX.t[:, lc], bank.t[:], AF.Identity, bias=ppc("bin", l * 72 + 46 + c)), reads=[bank.b, ppt.b], writes=[GX.b])
                            jst[k] = (A, U, GX, Zb, trs, Qb)

                        def stage1b(k):
                            dr, sp = jobs[k]
                            c0 = sp * SPW
                            A, U, GX, Zb, trs, Qb = jst[k]
                            pidx = l * 20 + dr * 10 + c
                            for pp_ in range(PPS):
                                cols = slice(c0 + pp_ * TG, c0 + (pp_ + 1) * TG)
                                lc = slice(pp_ * TG, (pp_ + 1) * TG)
                                Tr, Ti = trs[pp_]
                                S.op("act", ACT(A.t[:, lc], Tr.t[:], AF.Exp, bias=dpc("lam", pidx), scale=dpc("lam", pidx)), reads=[Tr.b, dpt.b], writes=[A.b])
                                S.op("dve", STT(U.t[:, lc], Ti.t[:], 1.0, xc[:, cols], ALU.add, ALU.mult), reads=[Ti.b, xcB[sp]], writes=[U.b])
                            S.op("pool", TT(Qb.t[:], A.t[:], A.t[:], ALU.mult), reads=[A.b], writes=[Qb.b])
                            if dr == 1:
                                S.op("pool", TT(Zb.t[:], GX.t[:], GX.t[:], ALU.mult), reads=[GX.b], writes=[Zb.b])
                                S.op("pool", TS(Zb.t[:], Zb.t[:], 0.044715, 1.0, ALU.mult, ALU.add), reads=[Zb.b], writes=[Zb.b])
                                S.op("pool", TT(Zb.t[:], Zb.t[:], GX.t[:], ALU.mult), reads=[Zb.b, GX.b], writes=[Zb.b])

                        def stage2a(k):
                            dr, sp = jobs[k]
                            A, U, GX, Zb, trs, Qb = jst[k]
                            S.op("act", ACT(Qb.t[:], Qb.t[:], AF.Sqrt, bias=qb.t[:, 0:1], scale=-0.0625), reads=[Qb.b, qb.b], writes=[Qb.b])

                        def stage2a_gelu(k):
                            dr, sp = jobs[k]
                            A, U, GX, Zb, trs, Qb = jst[k]
                            if dr == 1:
                                S.op("act", ACT(Zb.t[:], Zb.t[:], AF.Tanh, scale=GELU_C), reads=[Zb.b], writes=[Zb.b])

                        def stage2b(k):
                            dr, sp = jobs[k]
                            A, U, GX, Zb, trs, Qb = jst.pop(k)
                            cols = slice(sp * SPW, (sp + 1) * SPW)
                            S.op("dve", TT(U.t[:], Qb.t[:], U.t[:], ALU.mult), reads=[Qb.b, U.b], writes=[U.b])
                            if dr == 0:
                                init = 0.0 if sp == 0 else hf.t[:, sp * SPW - 1:sp * SPW]
                                S.op("dve", lambda h: h.tensor_tensor_scan(
                                    out=hf.t[:, cols], data0=A.t[:], data1=U.t[:], initial=init, op0=ALU.mult, op1=ALU.add),
                                    reads=[A.b, U.b, hf.b], writes=[hf.b])
                            else:
                                first = (sp == NSP - 1)
                                hi_prev = hini[(sp + 1) % 2]
                                hi_cur = hini[sp % 2]
                                init = 0.0 if first else hi_prev.t[:, 0:1]
                                rd = [A.b, U.b] + ([] if first else [hi_prev.b])
                                S.op("dve", lambda h: h.tensor_tensor_scan(
                                    out=HB.t[:, ::-1], data0=A.t[:, ::-1], data1=U.t[:, ::-1], initial=init, op0=ALU.mult, op1=ALU.add),
                                    reads=rd, writes=[HB.b])
                                S.op("dve", CP(hi_cur.t[:, 0:1], HB.t[:, 0:1]), reads=[HB.b], writes=[hi_cur.b])
                                S.op("dve", STT(Zb.t[:], Zb.t[:], 1.0, GX.t[:], ALU.add, ALU.mult), reads=[Zb.b, GX.b], writes=[Zb.b])
                                S.op("dve", TT(HS.t[:], hf.t[:, cols], HB.t[:], ALU.add), reads=[hf.b, HB.b], writes=[HS.b])
                                yp = YB[cnt["yb"] % 2]
                                cnt["yb"] += 1
                                S.op("dve", TT(yp.t[:], HS.t[:], Zb.t[:], ALU.mult), reads=[HS.b, Zb.b], writes=[yp.b])
                                S.dma("sp", yb_d[:, c, cols], yp.t[:], reads=[yp.b], writes=[ybB[c]])

                        for k in range(len(jobs) + 1):
                            pend_conv = []
                            if k < len(jobs):
                                stage1a(k)
                                if k == 0 and NSP > 1:
                                    pend_conv.append((c, 1))
                                if k + 2 < len(jobs) and jobs[k + 2][0] == 0:
                                    pend_conv.append((c, jobs[k + 2][1]))
                                for pc_ in pend_conv:
                                    conv_mm(*pc_)
                            if k >= 1:
                                stage2a(k - 1)
                            if k < len(jobs):
                                stage1b(k)
                            if k >= 1:
                                stage2a_gelu(k - 1)
                            if k < len(jobs):
                                for pc_ in pend_conv:
                                    conv_ev(*pc_)
                                if k == NSP and c + 1 < 10:
                                    xr_proj(c + 1, Wc[c + 1][0])
                                    conv_diag(c + 1)
                                if k == len(jobs) - 1 and c + 1 < 10:
                                    conv_sp(c + 1, 0)
                            if k >= 1:
                                stage2b(k - 1)
                    S.barrier()

                with ExitStack() as stk:
                    alloc_wbufs(stk, 2048)
                    st = alloc_ac(stk)
                    yab_ = TB(sb(stk, "yag", [128, 4, G], BF16))
                    ybb_ = TB(sb(stk, "ybg", [128, 10, G], BF16))
                    mg = TB(sb(stk, "mg", [128, KC, G], BF16))
                    mgB = [Buf() for _ in range(KC)]
                    Tga = [TB(sb(stk, "Tga", [128, TG], F32)) for _ in range(2)]
                    Tgb = [TB(sb(stk, "Tgb", [128, TG], F32)) for _ in range(2)]
                    m1_ = TB(sb(stk, "m1", [128, TG], F32))
                    m2_ = TB(sb(stk, "m2", [128, TG], F32))
                    m1 = [m1_, m1_]
                    m2 = [m2_, m2_]
                    ci = 0
                    def c_loads(gj):
                        gsl_ = slice(gj * G, (gj + 1) * G)
                        S.dma("sp", yab_.t[:], ya_d[:, :, gsl_], reads=yaB, writes=[yab_.b])
                        S.dma("sp", ybb_.t[:], yb_d[:, :, gsl_], reads=ybB, writes=[ybb_.b])
                        S.dma("sp", st["xgs"][gj % 2][:], xres_d[:, :, gsl_], reads=[xresB[gj]], writes=st["xgBs"][gj % 2])

                    c_loads(0)
                    for gi in range(NG):
                        xg, xgB = st["xgs"][gi % 2], st["xgBs"][gi % 2]
                        gsl = slice(gi * G, (gi + 1) * G)
                        for m in range(8):
                            wpa_ = load_w(wb["wpa"][l, m], 512, ("wpa", l, m))
                            wpb_ = load_w(wb["wpb"][l, m], 1280, ("wpb", l, m))
                            wga_ = load_w(wb["win"][l, 56 + m], 1024, ("win", l, 56 + m))
                            wgb_ = load_w(wb["win"][l, 64 + m], 1024, ("win", l, 64 + m))
                            for t in range(G // TG):
                                cols = slice(t * TG, (t + 1) * TG)
                                gcols = slice(gi * G + t * TG, gi * G + (t + 1) * TG)
                                b_pa, b_pb, b_ga, b_gb = S.bank(), S.bank(), S.bank(), S.bank()
                                S.mm([MM(b_pa.t[:], wpa_.t[:, kc * 128:(kc + 1) * 128], yab_.t[:, kc, cols], kc == 0, kc == 3) for kc in range(4)],
                                     reads=[wpa_.b, yab_.b], writes=[b_pa.b])
                                S.mm([MM(b_pb.t[:], wpb_.t[:, kc * 128:(kc + 1) * 128], ybb_.t[:, kc, cols], kc == 0, kc == 9) for kc in range(10)],
                                     reads=[wpb_.b, ybb_.b], writes=[b_pb.b])
                                S.mm([MM(b_ga.t[:], wga_.t[:, kc * 128:(kc + 1) * 128], xn2[:, kc, gcols], kc == 0, kc == KC - 1) for kc in range(KC)],
                                     reads=[wga_.b, xn2B[gi]], writes=[b_ga.b])
                                S.mm([MM(b_gb.t[:], wgb_.t[:, kc * 128:(kc + 1) * 128], xn2[:, kc, gcols], kc == 0, kc == KC - 1) for kc in range(KC)],
                                     reads=[wgb_.b, xn2B[gi]], writes=[b_gb.b])
                                ta, tb_, a1, a2_ = Tga[ci % 2], Tgb[ci % 2], m1[ci % 2], m2[ci % 2]
                                ci += 1
                                S.op("act", ACT(ta.t[:], b_ga.t[:], AF.Tanh, bias=dpc("bin", l * 72 + 56 + m), scale=0.5), reads=[b_ga.b, dpt.b], writes=[ta.b])
                                S.op("act", ACT(tb_.t[:], b_gb.t[:], AF.Tanh, bias=dpc("bin", l * 72 + 64 + m), scale=0.5), reads=[b_gb.b, dpt.b], writes=[tb_.b])
                                S.op("dve", STT(a1.t[:], ta.t[:], 1.0, b_pa.t[:], ALU.add, ALU.mult), reads=[ta.b, b_pa.b], writes=[a1.b])
                                S.op("dve", STT(a2_.t[:], tb_.t[:], 1.0, b_pb.t[:], ALU.add, ALU.mult), reads=[tb_.b, b_pb.b], writes=[a2_.b])
                                S.op("dve", TT(mg.t[:, m, cols], a1.t[:], a2_.t[:], ALU.add), reads=[a1.b, a2_.b], writes=[mgB[m]])
                        pend = None
                        for mo in range(8):
                            wo_ = load_w(wb["wo"][l, mo], 1024, ("wo", l, mo))
                            for t in range(G // TG):
                                cols = slice(t * TG, (t + 1) * TG)
                                by = S.bank()
                                S.mm([MM(by.t[:], wo_.t[:, kc * 128:(kc + 1) * 128], mg.t[:, kc, cols], kc == 0, kc == KC - 1) for kc in range(KC)],
                                     reads=[wo_.b] + mgB, writes=[by.b])
                                S.op("dve", STT(xg[:, mo, cols], by.t[:], 0.5, xg[:, mo, cols], ALU.mult, ALU.add), reads=[by.b, xgB[mo]], writes=[xgB[mo]])
                                nsq = norm_sq(xg, xgB, mo, cols, st)
                                if pend is not None:
                                    norm_mm(*pend)
                                pend = nsq
                        norm_mm(*pend)
                        if gi + 1 < NG:
                            c_loads(gi + 1)
                        ffn(l, 1, xg, xgB, G, st, True)
                        if l + 1 < L:
                            phase_a(l + 1, gi, st, True)
                        else:
                            for t in range(G // TG):
                                cols = slice(t * TG, (t + 1) * TG)
                                norm_tile(xg, xgB, cols, "nf", 0, lambda kc: xg[:, kc, cols], lambda kc: xgB[kc], st, True)
                            for tt in range(G // 128):
                                io = st["io"][st["ioi"] % 2]
                                st["ioi"] += 1
                                for half in range(2):
                                    bank = S.bank()
                                    S.mm([TR(bank.t[:, j * 128:(j + 1) * 128], xg[:, half * 4 + j, tt * 128:(tt + 1) * 128], identf.t[:])
                                          for j in range(4)], reads=xgB + [identf.b], writes=[bank.b])
                                    S.op("act", ACT(io.t[:, half * 512:(half + 1) * 512], bank.t[:], AF.Copy), reads=[bank.b], writes=[io.b])
                                r0 = tok0 + gi * G + tt * 128
                                S.dma("pool", y_d[r0:r0 + 128, :], io.t[:], reads=[io.b], writes=[Buf()])
                    S.barrier()
        S.barrier()
    return nc


def _blk(w, kcn):
    K, N = w.shape
    return np.ascontiguousarray(w.reshape(kcn, 128, N // 128, 128).transpose(2, 1, 0, 3)).reshape(N // 128, 128, kcn * 128)


def prep_weights(inp, L):
    f32 = np.float32
    wgu = np.empty((L, 2, NFF, 128, 2 * KC * 128), f32)
    wd = np.empty((L, 2, 8, 2, 128, 11 * 128), f32)
    win = np.empty((L, 72, 128, KC * 128), f32)
    wrg = np.empty((L, 10, 128, 4 * 128), f32)
    wpa = np.empty((L, 8, 128, 4 * 128), f32)
    wpb = np.empty((L, 8, 128, 10 * 128), f32)
    wo = np.empty((L, 8, 128, 8 * 128), f32)
    for l in range(L):
        for f, pre in enumerate(("ffn1", "ffn2")):
            wgu[l, f, :, :, 0:1024] = _blk(np.asarray(inp[pre + "_w_gate"][l]), KC)
            wgu[l, f, :, :, 1024:2048] = _blk(np.asarray(inp[pre + "_w_up"][l]), KC)
            dwn = np.asarray(inp[pre + "_w_down"][l])
            for half in range(2):
                wd[l, f, :, half] = _blk(dwn[half * 1408:(half + 1) * 1408], 11)
        win[l] = _blk(np.asarray(inp["w_in"][l]), KC)
        wa = np.asarray(inp["rg_w_a"][l])
        wx = np.asarray(inp["rg_w_x"][l])
        for dr in range(2):
            wrg[l, :, :, (2 * dr) * 128:(2 * dr + 1) * 128] = wa[dr]
            wrg[l, :, :, (2 * dr + 1) * 128:(2 * dr + 2) * 128] = wx[dr]
        wpa[l] = _blk(np.asarray(inp["w_proj_a"][l]), 4)
        wpb[l] = _blk(np.asarray(inp["w_proj_b"][l]), 10)
        wo[l] = _blk(np.asarray(inp["w_out"][l]), 8)
    PO, NPP = pp_layout(L)
    pp = np.zeros((128, NPP), f32)

    def put(name, arr):
        a = np.asarray(arr, f32)
        a = a.reshape(-1, a.shape[-1] // 128, 128)
        a = a.transpose(2, 0, 1).reshape(128, -1)
        pp[:, PO[name]:PO[name] + a.shape[1]] = a

    put("n1", np.asarray(inp["ffn1_norm"])[:L])
    put("nm", np.asarray(inp["mix_norm"])[:L])
    put("n2", np.asarray(inp["ffn2_norm"])[:L])
    put("nf", np.asarray(inp["final_norm"])[None, :])
    put("bin", np.asarray(inp["b_in"])[:L])
    put("cw", np.asarray(inp["conv_w"])[:L].reshape(L * 4, 1280))
    put("cb", np.asarray(inp["conv_b"])[:L])
    put("ba", np.asarray(inp["rg_b_a"])[:L].reshape(L * 2, 1280))
    put("bx", np.asarray(inp["rg_b_x"])[:L].reshape(L * 2, 1280))
    put("lam", np.asarray(inp["rg_lambda"])[:L].reshape(L * 2, 1280))
    return {"wgu": wgu, "wd": wd, "win": win, "wrg": wrg, "wpa": wpa, "wpb": wpb, "wo": wo, "pp": pp,
            "b_in": np.ascontiguousarray(np.asarray(inp["b_in"], f32)[:L])}


def kernel(**inputs):
    xp = np.asarray(inputs["x_prompt"], np.float32)
    xs = np.asarray(inputs["x_sample"], np.float32)
    L = DEPTH
    B1, S1, _ = xp.shape
    B2, S2, _ = xs.shape
    n1 = B1 // NCORES
    n2 = B2 // NCORES
    seq_lens = [S1] * n1 + [S2] * n2
    wts = prep_weights(inputs, L)
    nc = build(seq_lens, L)
    in_maps = []
    for c in range(NCORES):
        xc = np.concatenate([xp[c * n1:(c + 1) * n1].reshape(-1, D), xs[c * n2:(c + 1) * n2].reshape(-1, D)], axis=0)
        m = {"x": np.ascontiguousarray(xc)}
        m.update(wts)
        in_maps.append(m)
    res = run_bass_kernel_spmd(nc, in_maps, core_ids=list(range(NCORES)))
    yp = np.empty_like(xp)
    ys = np.empty_like(xs)
    for c in range(NCORES):
        y = res.results[c]["y"]
        yp[c * n1:(c + 1) * n1] = y[:n1 * S1].reshape(n1, S1, D)
        ys[c * n2:(c + 1) * n2] = y[n1 * S1:].reshape(n2, S2, D)
    return (yp, ys)
```

```python
import numpy as np
from contextlib import ExitStack
import concourse.bass as bass
import concourse.mybir as mybir
from concourse.bass_utils import run_bass_kernel_spmd

F32 = mybir.dt.float32
BF16 = mybir.dt.bfloat16
I32 = mybir.dt.int32
AF = mybir.ActivationFunctionType
ALU = mybir.AluOpType

D = 1024
DFF = 2816
NFF = 22
KC = 8
INC = 9216
DEPTH = 4
NCORES = 8
GROUPS = ((128, 1), (512, 4), (2048, 16))
SLOPES = [2.0 ** (-8.0 * h / 12.0) for h in range(1, 13)]
EPS = 1e-6
GELU_C = 0.7978845608028654
TG = 512


def pp_layout(L):
    off = {}
    n = 0
    for name, cnt in (("n1", L * 8), ("nm", L * 8), ("n2", L * 8), ("nf", 8), ("bin", L * 72),
                      ("cw", L * 40), ("cb", L * 10), ("ba", L * 20), ("bx", L * 20), ("lam", L * 20)):
        off[name] = n
        n += cnt
    return off, n


class Buf:
    __slots__ = ("w", "r")

    def __init__(self):
        self.w = None
        self.r = {}


class TB:
    __slots__ = ("t", "b")

    def __init__(self, t, b=None):
        self.t = t
        self.b = b if b is not None else Buf()


class Eng:
    def __init__(self, name, h, key):
        self.name = name
        self.h = h
        self.key = key
        self.cnt = 0
        self.waited = {}
        self.dkeys = []
        self.dma_n = 0


class Sched:
    def __init__(self, nc):
        self.nc = nc
        self.sems = {}
        self.E = {}
        self.banks = []
        self.bank_i = 0

    def add_engine(self, name, h, sem, dsems=()):
        key = "e_" + name
        self.sems[key] = sem
        e = Eng(name, h, key)
        for i, s in enumerate(dsems):
            k = "d_%s_%d" % (name, i)
            self.sems[k] = s
            e.dkeys.append(k)
        self.E[name] = e

    def _need(self, e, tok):
        key, val = tok
        if e.waited.get(key, 0) >= val:
            return
        if key == e.key and e.name == "pe":
            return
        e.h.wait_ge(self.sems[key], val)
        e.waited[key] = val

    def _deps(self, e, reads, writes):
        for b in reads:
            if b.w is not None:
                self._need(e, b.w)
        for b in writes:
            if b.w is not None:
                self._need(e, b.w)
            for k, v in b.r.items():
                self._need(e, (k, v))

    def _commit(self, tok, reads, writes):
        k, v = tok
        for b in reads:
            if b.r.get(k, 0) < v:
                b.r[k] = v
        for b in writes:
            b.w = tok
            b.r = {}

    def op(self, en, fn, reads=(), writes=()):
        e = self.E[en]
        self._deps(e, reads, writes)
        ins = fn(e.h)
        e.cnt += 1
        ins.then_inc(self.sems[e.key], 1)
        tok = (e.key, e.cnt)
        self._commit(tok, reads, writes)
        return tok

    def mm(self, fns, reads=(), writes=()):
        e = self.E["pe"]
        self._deps(e, reads, writes)
        ins = None
        for f in fns:
            ins = f(e.h)
        e.cnt += 1
        ins.then_inc(self.sems[e.key], 1)
        tok = (e.key, e.cnt)
        self._commit(tok, reads, writes)
        return tok

    def dma(self, qn, out, in_, reads=(), writes=()):
        q = self.E[qn]
        self._deps(q, reads, writes)
        K = len(q.dkeys)
        k = q.dma_n % K
        rnd = q.dma_n // K
        key = q.dkeys[k]
        if rnd > 0:
            self._need(q, (key, 16 * rnd))
        ins = q.h.dma_start(out=out, in_=in_)
        ins.then_inc(self.sems[key], 16)
        q.dma_n += 1
        tok = (key, 16 * (rnd + 1))
        self._commit(tok, reads, writes)
        return tok

    def barrier(self):
        toks = []
        for e in self.E.values():
            if e.cnt > 0:
                toks.append((e.key, e.cnt))
            K = len(e.dkeys)
            for k in range(K):
                n = (e.dma_n - k + K - 1) // K
                if n > 0:
                    toks.append((e.dkeys[k], 16 * n))
        for e in self.E.values():
            for t in toks:
                self._need(e, t)

    def bank(self):
        b = self.banks[self.bank_i % len(self.banks)]
        self.bank_i += 1
        return b


def MM(out, lhsT, rhs, start, stop):
    return lambda h: h.matmul(out, lhsT, rhs, start=start, stop=stop)


def TR(out, in_, ident):
    return lambda h: h.transpose(out, in_, ident)


def ACT(out, in_, func, bias=None, scale=None):
    kw = {}
    if bias is not None:
        kw["bias"] = bias
    if scale is not None:
        kw["scale"] = scale
    return lambda h: h.activation(out=out, in_=in_, func=func, **kw)


def STT(out, in0, scalar, in1, op0, op1):
    return lambda h: h.scalar_tensor_tensor(out=out, in0=in0, scalar=scalar, in1=in1, op0=op0, op1=op1)


def TT(out, in0, in1, op):
    return lambda h: h.tensor_tensor(out=out, in0=in0, in1=in1, op=op)


def TS(out, in0, s1, s2, op0, op1=None):
    if op1 is None:
        return lambda h: h.tensor_scalar(out=out, in0=in0, scalar1=s1, scalar2=None, op0=op0)
    return lambda h: h.tensor_scalar(out=out, in0=in0, scalar1=s1, scalar2=s2, op0=op0, op1=op1)


def CP(out, in_):
    return lambda h: h.tensor_copy(out=out, in_=in_)


def build(seq_lens, L, dbg=None):
    NTOK = sum(seq_lens)
    SMAXU = 4096
    SMAX = max(max(seq_lens), min(SMAXU, sum(seq_lens)))
    PO, NPP = pp_layout(L)
    nc = bass.Bass("TRN2", target_bir_lowering=False)
    x_d = nc.dram_tensor("x", [NTOK, D], F32, kind="ExternalInput").ap()
    y_d = nc.dram_tensor("y", [NTOK, D], F32, kind="ExternalOutput").ap()
    wshape = {
        "wgu": [L, 2, NFF, 128, 2 * KC * 128],
        "wd": [L, 2, 8, 2, 128, 11 * 128],
        "win": [L, 72, 128, KC * 128],
        "wrg": [L, 10, 128, 4 * 128],
        "wpa": [L, 8, 128, 4 * 128],
        "wpb": [L, 8, 128, 10 * 128],
        "wo": [L, 8, 128, 8 * 128],
    }
    wf = {k: nc.dram_tensor(k, s, F32, kind="ExternalInput").ap() for k, s in wshape.items()}
    wb = {k: nc.dram_tensor(k + "_b", s, BF16, kind="Internal").ap() for k, s in wshape.items()}
    pp_d = nc.dram_tensor("pp", [128, NPP], F32, kind="ExternalInput").ap()
    bin_d = nc.dram_tensor("b_in", [L, INC], F32, kind="ExternalInput").ap()
    xres_d = nc.dram_tensor("xres", [128, 8, SMAX], F32, kind="Internal").ap()
    ya_d = nc.dram_tensor("ya_s", [128, 4, SMAX], BF16, kind="Internal").ap()
    yb_d = nc.dram_tensor("yb_s", [128, 10, SMAX], BF16, kind="Internal").ap()
    e_d = nc.dram_tensor("e_s", [128, 12, 256], BF16, kind="Internal").ap()
    dbg_d = {}
    if dbg:
        for name, shp in dbg.items():
            dbg_d[name] = nc.dram_tensor("dbg_" + name, shp, F32, kind="ExternalOutput").ap()

    uid = [0]

    def sb(stack, name, shape, dt):
        uid[0] += 1
        return stack.enter_context(nc.sbuf_tensor("%s_%d" % (name, uid[0]), shape, dt))

    with ExitStack() as top:
        S = Sched(nc)
        nsem = [0]

        def newsem():
            nsem[0] += 1
            return top.enter_context(nc.semaphore("s%d" % nsem[0]))

        S.add_engine("pe", nc.tensor, newsem())
        S.add_engine("act", nc.scalar, newsem())
        S.add_engine("dve", nc.vector, newsem())
        S.add_engine("pool", nc.gpsimd, newsem(), [newsem() for _ in range(12)])
        S.add_engine("sp", nc.sync, newsem(), [newsem() for _ in range(12)])
        for i in range(7):
            S.banks.append(TB(top.enter_context(nc.psum_tensor("ps%d" % i, [128, 512], F32))))
        statbank = TB(top.enter_context(nc.psum_tensor("ps_stat", [128, 512], F32)))

        ppt = TB(sb(top, "pp", [128, NPP], F32))
        dpt = TB(sb(top, "dp", [128, NPP], F32))
        identf = TB(sb(top, "identf", [128, 128], F32))
        onesb = TB(sb(top, "onesb", [128, 128], BF16))
        eB = Buf()
        xn2 = sb(top, "xn2", [128, KC, SMAX], BF16)
        WBN = 6
        wbufs = []
        wbi = [0]

        def alloc_wbufs(stk, ncols):
            wbufs[:] = [TB(sb(stk, "wbuf%d" % i, [128, ncols], BF16)) for i in range(WBN)]

        def ppc(name, idx):
            return ppt.t[:, PO[name] + idx: PO[name] + idx + 1]

        def dpc(name, idx):
            return dpt.t[:, PO[name] + idx: PO[name] + idx + 1]

        S.dma("sp", ppt.t[:], pp_d[:, :], writes=[ppt.b])

        wB = {}

        def cast_jobs(l):
            jobs = []
            for f in range(2):
                for m in range(NFF):
                    jobs.append((wb["wgu"][l, f, m], wf["wgu"][l, f, m], [("wgu", l, f, m)]))
                for mo in range(8):
                    jobs.append((wb["wd"][l, f, mo], wf["wd"][l, f, mo], [("wd", l, f, mo)]))
            for m4 in range(18):
                jobs.append((wb["win"][l, 4 * m4:4 * m4 + 4], wf["win"][l, 4 * m4:4 * m4 + 4], [("win", l, 4 * m4 + j) for j in range(4)]))
            for key, nsplit in (("wrg", 1), ("wpa", 1), ("wpb", 2), ("wo", 2)):
                n = wshape[key][1]
                step = n // nsplit
                for s0 in range(0, n, step):
                    jobs.append((wb[key][l, s0:s0 + step], wf[key][l, s0:s0 + step], [(key, l, j) for j in range(s0, s0 + step)]))
            return jobs

        def issue_casts(jobs):
            for (o, i, keys) in jobs:
                b = Buf()
                S.dma("pool", o, i, writes=[b])
                for kk in keys:
                    wB[kk] = b

        def cast_layer(l):
            issue_casts(cast_jobs(l))

        cast_layer(0)

        def load_w(src_ap, ncols, bkey):
            w = wbufs[wbi[0] % WBN]
            wbi[0] += 1
            S.dma("sp", w.t[:, 0:ncols], src_ap, reads=[wB[bkey]], writes=[w.b])
            return w

        with ExitStack() as st0:
            idi = TB(sb(st0, "idi", [128, 256], I32))
            relf = TB(sb(st0, "relf", [128, 256], F32))
            absr = TB(sb(st0, "absr", [128, 256], F32))
            msk = TB(sb(st0, "msk", [128, 256], F32))
            tmpE = TB(sb(st0, "tmpE", [128, 256], F32))
            Et = TB(sb(st0, "E", [128, 12, 256], BF16))
            S.op("pool", lambda h: h.iota(idi.t[:, 0:128], pattern=[[-1, 128]], base=0, channel_multiplier=1),
                 writes=[idi.b])
            S.op("dve", CP(relf.t[:, 0:128], idi.t[:, 0:128]), reads=[idi.b], writes=[relf.b])
            S.op("dve", TS(identf.t[:], relf.t[:, 0:128], 0.0, None, ALU.is_equal), reads=[relf.b], writes=[identf.b])
            S.op("dve", lambda h: h.memset(onesb.t[:], 1.0), writes=[onesb.b])
            S.op("pool", lambda h: h.iota(idi.t[:], pattern=[[-1, 256]], base=64, channel_multiplier=1),
                 reads=[relf.b], writes=[idi.b])
            S.op("dve", CP(relf.t[:], idi.t[:]), reads=[idi.b], writes=[relf.b])
            S.op("act", ACT(absr.t[:], relf.t[:], AF.Abs), reads=[relf.b], writes=[absr.b])
            S.op("dve", TS(msk.t[:], absr.t[:], 64.5, None, ALU.is_le), reads=[absr.b], writes=[msk.b])
            for hh in range(12):
                d = GROUPS[hh // 4][1]
                S.op("act", ACT(tmpE.t[:], absr.t[:], AF.Exp, scale=-SLOPES[hh] * d), reads=[absr.b], writes=[tmpE.b])
                S.op("dve", TT(Et.t[:, hh, :], tmpE.t[:], msk.t[:], ALU.mult), reads=[tmpE.b, msk.b], writes=[Et.b])
            S.dma("sp", e_d[:, :, :], Et.t[:], reads=[Et.b], writes=[eB])
            S.op("dve", CP(dpt.t[:], ppt.t[:]), reads=[ppt.b], writes=[dpt.b])
            for name, cnt in (("bin", L * 72), ("ba", L * 20), ("bx", L * 20)):
                o = PO[name]
                S.op("dve", TS(dpt.t[:, o:o + cnt], ppt.t[:, o:o + cnt], 0.5, None, ALU.mult), reads=[ppt.b], writes=[dpt.b])
            o = PO["lam"]
            cnt = L * 20
            lt = TB(sb(st0, "lt", [128, cnt], F32))
            S.op("act", ACT(lt.t[:], ppt.t[:, o:o + cnt], AF.Exp, scale=-1.0), reads=[ppt.b], writes=[lt.b])
            S.op("act", ACT(lt.t[:], lt.t[:], AF.Ln, bias=1.0), reads=[lt.b], writes=[lt.b])
            S.op("dve", TS(dpt.t[:, o:o + cnt], lt.t[:], -4.0, None, ALU.mult), reads=[lt.b], writes=[dpt.b])
            S.barrier()

        def norm_sq(xg, xgB, kc, cols, st):
            sq = st["sq"][st["sqi"] % 3]
            st["sqi"] += 1
            S.op("act", ACT(sq.t[:], xg[:, kc, cols], AF.Square), reads=[xgB[kc]], writes=[sq.b])
            return (sq, kc)

        def norm_mm(sq, kc):
            S.mm([MM(statbank.t[:], onesb.t[:], sq.t[:], kc == 0, kc == KC - 1)], reads=[sq.b, onesb.b], writes=[statbank.b])

        def norm_stat(xg, xgB, kc, cols, st):
            norm_mm(*norm_sq(xg, xgB, kc, cols, st))

        def norm_fin(xg, xgB, cols, gname, gidx0, out_fn, outB, st):
            sd = st["sd"]
            S.op("act", ACT(sd.t[:], statbank.t[:], AF.Sqrt, bias=st["epsb"].t[:, 0:1], scale=1.0 / D), reads=[statbank.b, st["epsb"].b], writes=[sd.b])
            S.op("dve", lambda h: h.reciprocal(out=sd.t[:], in_=sd.t[:]), reads=[sd.b], writes=[sd.b])
            for kc in range(KC):
                S.op("dve", STT(out_fn(kc), xg[:, kc, cols], ppc(gname, gidx0 + kc), sd.t[:], ALU.mult, ALU.mult),
                     reads=[xgB[kc], sd.b, ppt.b], writes=[outB(kc) if callable(outB) else outB])

        def norm_tile(xg, xgB, cols, gname, gidx0, out_fn, outB, st, stats_done=False):
            if not stats_done:
                for kc in range(KC):
                    norm_stat(xg, xgB, kc, cols, st)
            norm_fin(xg, xgB, cols, gname, gidx0, out_fn, outB, st)

        def ffn(l, f, xg, xgB, G, st, stats_done=False):
            nt = G // TG
            assert nt == 1
            xn = st["xn"]
            for t in range(nt):
                cols = slice(t * TG, (t + 1) * TG)
                norm_tile(xg, xgB, cols, "n1" if f == 0 else "n2", l * 8, lambda kc: xn.t[:, kc, cols], lambda kc: st["xnB"][kc], st, stats_done)
            h = st["h"]
            for half in range(2):
                for mi in range(11):
                    m = half * 11 + mi
                    w = load_w(wb["wgu"][l, f, m], 2048, ("wgu", l, f, m))
                    for t in range(nt):
                        cols = slice(t * TG, (t + 1) * TG)
                        bg = S.bank()
                        bu = S.bank()
                        if m == 0:
                            for kc in range(KC):
                                S.mm([MM(bg.t[:], w.t[:, kc * 128:(kc + 1) * 128], xn.t[:, kc, cols], kc == 0, kc == KC - 1)],
                                     reads=[w.b, st["xnB"][kc]], writes=[bg.b])
                                S.mm([MM(bu.t[:], w.t[:, (8 + kc) * 128:(9 + kc) * 128], xn.t[:, kc, cols], kc == 0, kc == KC - 1)],
                                     reads=[w.b, st["xnB"][kc]], writes=[bu.b])
                        else:
                            S.mm([MM(bg.t[:], w.t[:, kc * 128:(kc + 1) * 128], xn.t[:, kc, cols], kc == 0, kc == KC - 1) for kc in range(KC)],
                                 reads=[w.b] + st["xnB"], writes=[bg.b])
                            S.mm([MM(bu.t[:], w.t[:, (8 + kc) * 128:(9 + kc) * 128], xn.t[:, kc, cols], kc == 0, kc == KC - 1) for kc in range(KC)],
                                 reads=[w.b] + st["xnB"], writes=[bu.b])
                        T = st["T"][st["Ti"] % 2]
                        W = st["W"][st["Ti"] % 2]
                        st["Ti"] += 1
                        S.op("act", ACT(T.t[:], bg.t[:], AF.Tanh, scale=0.5), reads=[bg.b], writes=[T.b])
                        S.op("dve", STT(W.t[:], T.t[:], 1.0, bg.t[:], ALU.add, ALU.mult), reads=[T.b, bg.b], writes=[W.b])
                        S.op("dve", TT(h.t[:, mi, cols], W.t[:], bu.t[:], ALU.mult), reads=[W.b, bu.b], writes=[st["hB"][mi]])
                pend = None
                for mo in range(8):
                    w = load_w(wb["wd"][l, f, mo, half], 1408, ("wd", l, f, mo))
                    for t in range(nt):
                        cols = slice(t * TG, (t + 1) * TG)
                        by = S.bank()
                        S.mm([MM(by.t[:], w.t[:, kc * 128:(kc + 1) * 128], h.t[:, kc, cols], kc == 0, kc == 10) for kc in range(11)],
                             reads=[w.b] + st["hB"], writes=[by.b])
                        S.op("dve", STT(xg[:, mo, cols], by.t[:], 0.25, xg[:, mo, cols], ALU.mult, ALU.add),
                             reads=[by.b, xgB[mo]], writes=[xgB[mo]])
                        if half == 1:
                            nsq = norm_sq(xg, xgB, mo, cols, st)
                            if pend is not None:
                                norm_mm(*pend)
                            pend = nsq
                if pend is not None:
                    norm_mm(*pend)

        units = []
        i_ = 0
        while i_ < len(seq_lens):
            nb_ = 1
            while (i_ + nb_ < len(seq_lens) and seq_lens[i_ + nb_] == seq_lens[i_] and (nb_ + 1) * seq_lens[i_] <= SMAXU):
                nb_ += 1
            units.append((sum(seq_lens[:i_]), seq_lens[i_], nb_))
            i_ += nb_
        for si, (tok0, SL, NB) in enumerate(units):
            ST = NB * SL
            G = TG
            NG = ST // G
            xn2B = [Buf() for _ in range(NG)]
            xresB = [Buf() for _ in range(NG)]
            yaB = [Buf() for _ in range(4)]
            ybB = [Buf() for _ in range(10)]

            def alloc_ac(stk):
                st = {}
                st["xgs"] = [sb(stk, "xg", [128, KC, G], F32) for _ in range(2)]
                st["xgBs"] = [[Buf() for _ in range(KC)] for _ in range(2)]
                st["xn"] = TB(sb(stk, "xn", [128, KC, G], BF16))
                st["xnB"] = [Buf() for _ in range(KC)]
                st["h"] = TB(sb(stk, "h", [128, 11, G], BF16))
                st["hB"] = [Buf() for _ in range(11)]
                st["sq"] = [TB(sb(stk, "sq", [128, TG], BF16)) for _ in range(3)]
                st["sqi"] = 0
                st["sd"] = TB(sb(stk, "sd", [128, TG], F32))
                st["T"] = [TB(sb(stk, "T", [128, TG], F32)) for _ in range(2)]
                st["W"] = [TB(sb(stk, "W", [128, TG], F32)) for _ in range(2)]
                st["Ti"] = 0
                st["epsb"] = TB(sb(stk, "epsb", [128, 1], F32))
                S.op("dve", lambda h: h.memset(st["epsb"].t[:], EPS), writes=[st["epsb"].b])
                one_io = TB(sb(stk, "io", [128, D], F32))
                st["io"] = [one_io, one_io]
                st["ioi"] = 0
                return st

            def phase_a(l, gi, st, stats_done=False, store_q="pool"):
                xg, xgB = st["xgs"][gi % 2], st["xgBs"][gi % 2]
                ffn(l, 0, xg, xgB, G, st, stats_done)
                for t in range(G // TG):
                    cols = slice(t * TG, (t + 1) * TG)
                    gcols = slice(gi * G + t * TG, gi * G + (t + 1) * TG)
                    norm_tile(xg, xgB, cols, "nm", l * 8, lambda kc: xn2[:, kc, gcols], xn2B[gi], st, True)
                S.dma(store_q, xres_d[:, :, gi * G:(gi + 1) * G], xg[:], reads=xgB, writes=[xresB[gi]])

            with ExitStack() as stk:
                alloc_wbufs(stk, 2048)
                st = alloc_ac(stk)
                for gi in range(NG):
                    xg, xgB = st["xgs"][gi % 2], st["xgBs"][gi % 2]
                    for tt in range(G // 128):
                        io = st["io"][st["ioi"] % 2]
                        st["ioi"] += 1
                        r0 = tok0 + gi * G + tt * 128
                        S.dma("sp", io.t[:], x_d[r0:r0 + 128, :], writes=[io.b])
                        for half in range(2):
                            bank = S.bank()
                            S.mm([TR(bank.t[:, j * 128:(j + 1) * 128], io.t[:, (half * 4 + j) * 128:(half * 4 + j + 1) * 128], identf.t[:])
                                  for j in range(4)], reads=[io.b, identf.b], writes=[bank.b])
                            for j in range(4):
                                kc = half * 4 + j
                                S.op("act", ACT(xg[:, kc, tt * 128:(tt + 1) * 128], bank.t[:, j * 128:(j + 1) * 128], AF.Copy),
                                     reads=[bank.b], writes=[xgB[kc]])
                    phase_a(0, gi, st, False, "sp" if si == 0 else "pool")
                S.barrier()

            for l in range(L):
                with ExitStack() as stk:
                    alloc_wbufs(stk, 1024)
                    bvbc = TB(sb(stk, "bvbc", [128, 1536], F32))
                    Et = TB(sb(stk, "E", [128, 12, 256], BF16))
                    S.dma("sp", Et.t[:], e_d[:, :, :], reads=[eB], writes=[Et.b])
                    S.dma("sp", bvbc.t[:], bin_d[l:l + 1, 3072:4608].partition_broadcast(128), writes=[bvbc.b])
                    qT = [TB(sb(stk, "qT", [128, ST], BF16)) for _ in range(2)]
                    kT = [TB(sb(stk, "kT", [128, ST], BF16)) for _ in range(2)]
                    Vt = [TB(sb(stk, "Vt", [128, ST // 128, 128], BF16)) for _ in range(2)]
                    acc = TB(sb(stk, "acc", [128, 2, ST], F32))
                    Xb = [TB(sb(stk, "Xb", [128, 256], F32)) for _ in range(4)]
                    Pb = [TB(sb(stk, "Pb", [128, 256], BF16)) for _ in range(10)]
                    accB = [Buf() for _ in range(4)]
                    fz = TB(sb(stk, "fz", [128, 2], F32))
                    yab = TB(sb(stk, "yab", [128, ST], BF16))
                    heads = [(hs, g) for hs in range(4) for g in range(3)]

                    def load_head(hs, g):
                        hh = 4 * g + hs
                        return [load_w(wb["win"][l, mm_], 1024, ("win", l, mm_)) for mm_ in (hh, 12 + hh, 24 + hh)]

                    nxt = load_head(*heads[0])
                    for hi, (hs, g) in enumerate(heads):
                        hh = 4 * g + hs
                        d = GROUPS[g][1]
                        Lr = SL // d
                        nkt = Lr // 128
                        wq, wk, wv = nxt
                        if hi + 1 < len(heads):
                            nxt = load_head(*heads[hi + 1])
                        q = qT[hi % 2]
                        k = kT[hi % 2]
                        V = Vt[hi % 2]
                        qvs = [q.t[:, b_ * SL:(b_ + 1) * SL].rearrange("p (r m) -> p r m", r=d) for b_ in range(NB)]
                        kvs = [k.t[:, b_ * SL:(b_ + 1) * SL].rearrange("p (r m) -> p r m", r=d) for b_ in range(NB)]
                        for (wt, dsts, dstB, mcol) in ((wq, qvs, q.b, hh), (wk, kvs, k.b, 12 + hh)):
                            for ti in range(ST // TG):
                                bank = S.bank()
                                S.mm([MM(bank.t[:], wt.t[:, kc * 128:(kc + 1) * 128], xn2[:, kc, ti * TG:(ti + 1) * TG], kc == 0, kc == KC - 1)
                                      for kc in range(KC)], reads=[wt.b, xn2B[ti * TG // G]], writes=[bank.b])
                                b_ = (ti * TG) // SL
                                m0 = (ti * TG - b_ * SL) // d
                                S.op("act", ACT(dsts[b_][:, :, m0:m0 + TG // d], bank.t[:].rearrange("p (m r) -> p r m", r=d), AF.Identity,
                                                bias=ppc("bin", l * 72 + mcol)), reads=[bank.b, ppt.b], writes=[dstB])
                        ntile = d * nkt
                        for b_ in range(NB):
                            xv = xn2[:, :, b_ * SL:(b_ + 1) * SL].rearrange("p c (m r) -> p c r m", r=d)
                            for t4 in range(0, ntile, 4):
                                bank = S.bank()
                                fns = []
                                for j in range(4):
                                    r, kt = divmod(t4 + j, nkt)
                                    for kc in range(KC):
                                        fns.append(MM(bank.t[:, j * 128:(j + 1) * 128], xv[:, kc, r, kt * 128:(kt + 1) * 128],
                                                      wv.t[:, kc * 128:(kc + 1) * 128], kc == 0, kc == KC - 1))
                                S.mm(fns, reads=[wv.b] + xn2B, writes=[bank.b])
                                S.op("dve", TT(V.t[:, b_ * ntile + t4:b_ * ntile + t4 + 4, :], bank.t[:].rearrange("p (j c) -> p j c", j=4),
                                               bvbc.t[:, hh * 128:(hh + 1) * 128].unsqueeze(1).to_broadcast([128, 4, 128]), ALU.add),
                                     reads=[bank.b, bvbc.b], writes=[V.b])
                        accvs = [acc.t[:, :, b_ * SL:(b_ + 1) * SL].rearrange("p u (m r) -> p u r m", r=d) for b_ in range(NB)]
                        S.op("dve", lambda h: h.memset(fz.t[:, 0:1], 0.0), reads=accB, writes=accB + [fz.b])
                        its = [(b_, r, kt) for b_ in range(NB) for r in range(d) for kt in range(nkt + 1)]
                        Pof = {}

                        def stage1(idx):
                            b_, r, kt = its[idx]
                            kv = kvs[b_]
                            qv = qvs[b_]
                            if kt >= nkt:
                                return
                            qs = max(0, 128 * kt - 64)
                            qe = min(Lr, 128 * kt + 192)
                            c0 = qs - (128 * kt - 64)
                            ncol = qe - qs
                            bank = S.bank()
                            S.mm([MM(bank.t[:, 0:ncol], kv[:, r, kt * 128:(kt + 1) * 128], qv[:, r, qs:qe], True, True)],
                                 reads=[k.b, q.b], writes=[bank.b])
                            X = Xb[idx % len(Xb)]
                            P = Pb[idx % len(Pb)]
                            S.op("act", ACT(X.t[:, 0:ncol], bank.t[:, 0:ncol], AF.Exp, scale=128.0 ** -0.5), reads=[bank.b], writes=[X.b])
                            S.op("pool", TT(P.t[:, c0:c0 + ncol], X.t[:, 0:ncol], Et.t[:, hh, c0:c0 + ncol], ALU.mult),
                                 reads=[X.b, Et.b], writes=[P.b])
                            Pof[(b_, r, kt)] = P

                        def stage2(idx):
                            b_, r, kt = its[idx]
                            accv = accvs[b_]
                            j = kt
                            a = max(0, 128 * j - 64)
                            bnd = min(Lr, 128 * j + 64)
                            n = bnd - a
                            srcs = [(Pof[(b_, r, kk)], kk) for kk in (kt - 1, kt) if 0 <= kk < nkt]
                            bank = S.bank()
                            fns = []
                            for which in range(2):
                                for si_, (Pm, ktp) in enumerate(srcs):
                                    cc = a - (128 * ktp - 64)
                                    lhs = V.t[:, b_ * ntile + r * nkt + ktp, :] if which == 0 else onesb.t[:]
                                    fns.append(MM(bank.t[:, which * 128:which * 128 + n], lhs, Pm.t[:, cc:cc + n], si_ == 0, si_ == len(srcs) - 1))
                            S.mm(fns, reads=[V.b, onesb.b] + [s_[0].b for s_ in srcs], writes=[bank.b])
                            src = bank.t[:, 0:256].rearrange("p (u c) -> p u c", u=2)[:, :, 0:n]
                            dst = accv[:, :, r, a:bnd]
                            ab = accB[idx % 4]
                            if g == 0:
                                S.op("act", ACT(dst, src, AF.Copy), reads=[bank.b], writes=[ab])
                            else:
                                S.op("dve", TT(dst, src, dst, ALU.add), reads=[bank.b, ab], writes=[ab])

                        LEAD = 5
                        for idx in range(len(its) + LEAD):
                            if idx < len(its):
                                stage1(idx)
                            if idx >= LEAD:
                                stage2(idx - LEAD)
                        if g == 2:
                            S.op("dve", lambda h: h.reciprocal(out=acc.t[:, 1, :], in_=acc.t[:, 1, :]), reads=accB, writes=accB)
                            S.op("dve", TT(yab.t[:], acc.t[:, 0, :], acc.t[:, 1, :], ALU.mult), reads=accB, writes=[yab.b])
                            S.dma("sp", ya_d[:, hs, 0:ST], yab.t[:], reads=[yab.b], writes=[yaB[hs]])
                    S.barrier()

                with ExitStack() as stk:
                    alloc_wbufs(stk, 1024)
                    SPW = 1024
                    NSP = SL // SPW
                    NF = NB * NSP
                    PPS = SPW // TG
                    xr = TB(sb(stk, "xr", [128, NB * (SL + 4)], F32))
                    xc = sb(stk, "xc", [128, ST], F32)
                    xcb = sb(stk, "xcb", [128, ST], BF16)
                    xcB = [Buf() for _ in range(NF)]
                    xcbB = [Buf() for _ in range(NF)]
                    hf = TB(sb(stk, "hf", [128, ST], F32))
                    Ab = [TB(sb(stk, "A", [128, SPW], F32)) for _ in range(2)]
                    Ub = [TB(sb(stk, "U", [128, SPW], F32)) for _ in range(2)]
                    Qbs = [TB(sb(stk, "Q", [128, SPW], F32)) for _ in range(2)]
                    HB = TB(sb(stk, "HB", [128, SPW], F32))
                    HS = TB(sb(stk, "HS", [128, SPW], F32))
                    hini = [TB(sb(stk, "hini", [128, 1], F32)) for _ in range(2)]
                    GXb = [TB(sb(stk, "GX", [128, SPW], F32)) for _ in range(2)]
                    Zbb = [TB(sb(stk, "Z", [128, SPW], F32)) for _ in range(2)]
                    YB = [TB(sb(stk, "YB", [128, SPW], BF16)) for _ in range(2)]
                    Trb = [TB(sb(stk, "Tr", [128, TG], F32)) for _ in range(2)]
                    Tib = [TB(sb(stk, "Ti", [128, TG], F32)) for _ in range(2)]
                    cnt = {"pc": 0, "yb": 0, "job": 0}
                    S.op("pool", lambda h: h.memset(xr.t[:], 0.0), writes=[xr.b])
                    qb = TB(sb(stk, "qb", [128, 1], F32))
                    S.op("dve", lambda h: h.memset(qb.t[:], 0.0625), writes=[qb.b])

                    def load_chunk(c):
                        return [load_w(wb["win"][l, 36 + c], 1024, ("win", l, 36 + c)),
                                load_w(wb["win"][l, 46 + c], 1024, ("win", l, 46 + c)),
                                load_w(wb["wrg"][l, c], 512, ("wrg", l, c))]

                    def xr_proj(c, wxr):
                        for ti in range(ST // TG):
                            bank = S.bank()
                            S.mm([MM(bank.t[:], wxr.t[:, kc * 128:(kc + 1) * 128], xn2[:, kc, ti * TG:(ti + 1) * TG], kc == 0, kc == KC - 1)
                                  for kc in range(KC)], reads=[wxr.b, xn2B[ti * TG // G]], writes=[bank.b])
                            b_ = (ti * TG) // SL
                            xo = b_ * (SL + 4) + 2 + (ti * TG - b_ * SL)
                            S.op("act", ACT(xr.t[:, xo:xo + TG], bank.t[:], AF.Identity, bias=ppc("bin", l * 72 + 36 + c)),
                                 reads=[bank.b, ppt.b], writes=[xr.b])

                    dgs = [TB(sb(stk, "dg", [128, 4, 128], F32)) for _ in range(2)]

                    def conv_diag(c):
                        dg = dgs[c % 2]
                        for jt in range(4):
                            S.op("pool", TS(dg.t[:, jt, :], identf.t[:], ppc("cw", l * 40 + jt * 10 + c), None, ALU.mult),
                                 reads=[identf.b, ppt.b], writes=[dg.b])

                    conv_banks = {}

                    def conv_mm(c, b_, sp):
                        dg = dgs[c % 2]
                        c0 = b_ * (SL + 4) + sp * SPW
                        bl = []
                        for pp_ in range(PPS):
                            t0 = c0 + pp_ * TG
                            bank = S.bank()
                            S.mm([MM(bank.t[:], dg.t[:, jt, :], xr.t[:, t0 + jt:t0 + jt + TG], jt == 0, jt == 3) for jt in range(4)],
                                 reads=[dg.b, xr.b], writes=[bank.b])
                            bl.append(bank)
                        conv_banks[(c, b_, sp)] = bl

                    def conv_ev(c, b_, sp):
                        c0 = b_ * SL + sp * SPW
                        gsp = b_ * NSP + sp
                        for pp_, bank in enumerate(conv_banks.pop((c, b_, sp))):
                            t0 = c0 + pp_ * TG
                            S.op("act", ACT(xc[:, t0:t0 + TG], bank.t[:], AF.Identity, bias=ppc("cb", l * 10 + c)), reads=[bank.b, ppt.b], writes=[xcB[gsp]])
                        S.op("pool", CP(xcb[:, c0:c0 + SPW], xc[:, c0:c0 + SPW]), reads=[xcB[gsp]], writes=[xcbB[gsp]])

                    def conv_sp(c, b_, sp):
                        conv_mm(c, b_, sp)
                        conv_ev(c, b_, sp)

                    Wc = [load_chunk(0)]
                    xr_proj(0, Wc[0][0])
                    conv_diag(0)
                    conv_sp(0, 0, 0)
                    pend_casts = cast_jobs(l + 1) if (si == 0 and l + 1 < L) else []
                    for c in range(10):
                        wxr, wgr, wrg = Wc[c]
                        if c + 1 < 10:
                            Wc.append(load_chunk(c + 1))
                        if pend_casts and c % 2 == 0:
                            npart = (len(pend_casts) + (4 - c // 2)) // (5 - c // 2)
                            issue_casts(pend_casts[:npart])
                            pend_casts = pend_casts[npart:]

                        jobs = ([(0, b_, sp) for b_ in range(NB) for sp in range(NSP)]
                                + [(1, b_, sp) for b_ in range(NB) for sp in range(NSP - 1, -1, -1)])
                        jst = {}

                        def stage1a(k):
                            dr, b_, sp = jobs[k]
                            c0 = b_ * SL + sp * SPW
                            gsp = b_ * NSP + sp
                            j = cnt["job"]
                            cnt["job"] += 1
                            A = Ab[j % 2]
                            U = Ub[j % 2]
                            Qb = Qbs[j % 2]
                            pidx = l * 20 + dr * 10 + c
                            trs = []
                            for pp_ in range(PPS):
                                cols = slice(c0 + pp_ * TG, c0 + (pp_ + 1) * TG)
                                Tr = Trb[cnt["pc"] % 2]
                                Ti = Tib[cnt["pc"] % 2]
                                cnt["pc"] += 1
                                ba_ = S.bank()
                                bx_ = S.bank()
                                S.mm([MM(ba_.t[:], wrg.t[:, (2 * dr) * 128:(2 * dr + 1) * 128], xcb[:, cols], True, True)], reads=[wrg.b, xcbB[gsp]], writes=[ba_.b])
                                S.mm([MM(bx_.t[:], wrg.t[:, (2 * dr + 1) * 128:(2 * dr + 2) * 128], xcb[:, cols], True, True)], reads=[wrg.b, xcbB[gsp]], writes=[bx_.b])
                                S.op("act", ACT(Tr.t[:], ba_.t[:], AF.Tanh, bias=dpc("ba", pidx), scale=0.5), reads=[ba_.b, dpt.b], writes=[Tr.b])
                                S.op("act", ACT(Ti.t[:], bx_.t[:], AF.Tanh, bias=dpc("bx", pidx), scale=0.5), reads=[bx_.b, dpt.b], writes=[Ti.b])
                                trs.append((Tr, Ti))
                            GX = Zb = None
                            if dr == 1:
                                GX = GXb[j % 2]
                                Zb = Zbb[j % 2]
                                for pp_ in range(PPS):
                                    pcols = slice(c0 + pp_ * TG, c0 + (pp_ + 1) * TG)
                                    lc = slice(pp_ * TG, (pp_ + 1) * TG)
                                    bank = S.bank()
                                    S.mm([MM(bank.t[:], wgr.t[:, kc * 128:(kc + 1) * 128], xn2[:, kc, pcols], kc == 0, kc == KC - 1) for kc in range(KC)],
                                         reads=[wgr.b, xn2B[(c0 + pp_ * TG) // G]], writes=[bank.b])
                                    S.op("act", ACT(GX.t[:, lc], bank.t[:], AF.Identity, bias=ppc("bin", l * 72 + 46 + c)), reads=[bank.b, ppt.b], writes=[GX.b])
                            jst[k] = (A, U, GX, Zb, trs, Qb)

                        def stage1b(k):
                            dr, b_, sp = jobs[k]
                            c0 = b_ * SL + sp * SPW
                            gsp = b_ * NSP + sp
                            A, U, GX, Zb, trs, Qb = jst[k]
                            pidx = l * 20 + dr * 10 + c
                            for pp_ in range(PPS):
                                cols = slice(c0 + pp_ * TG, c0 + (pp_ + 1) * TG)
                                lc = slice(pp_ * TG, (pp_ + 1) * TG)
                                Tr, Ti = trs[pp_]
                                S.op("act", ACT(A.t[:, lc], Tr.t[:], AF.Exp, bias=dpc("lam", pidx), scale=dpc("lam", pidx)), reads=[Tr.b, dpt.b], writes=[A.b])
                                S.op("dve", STT(U.t[:, lc], Ti.t[:], 1.0, xc[:, cols], ALU.add, ALU.mult), reads=[Ti.b, xcB[gsp]], writes=[U.b])
                            S.op("pool", TT(Qb.t[:], A.t[:], A.t[:], ALU.mult), reads=[A.b], writes=[Qb.b])
                            if dr == 1:
                                S.op("pool", TT(Zb.t[:], GX.t[:], GX.t[:], ALU.mult), reads=[GX.b], writes=[Zb.b])
                                S.op("pool", TS(Zb.t[:], Zb.t[:], 0.044715, 1.0, ALU.mult, ALU.add), reads=[Zb.b], writes=[Zb.b])
                                S.op("pool", TT(Zb.t[:], Zb.t[:], GX.t[:], ALU.mult), reads=[Zb.b, GX.b], writes=[Zb.b])

                        def stage2a(k):
                            dr, b_, sp = jobs[k]
                            A, U, GX, Zb, trs, Qb = jst[k]
                            S.op("act", ACT(Qb.t[:], Qb.t[:], AF.Sqrt, bias=qb.t[:, 0:1], scale=-0.0625), reads=[Qb.b, qb.b], writes=[Qb.b])

                        def stage2a_gelu(k):
                            dr, b_, sp = jobs[k]
                            A, U, GX, Zb, trs, Qb = jst[k]
                            if dr == 1:
                                S.op("act", ACT(Zb.t[:], Zb.t[:], AF.Tanh, scale=GELU_C), reads=[Zb.b], writes=[Zb.b])

                        def stage2b(k):
                            dr, b_, sp = jobs[k]
                            A, U, GX, Zb, trs, Qb = jst.pop(k)
                            c0 = b_ * SL + sp * SPW
                            cols = slice(c0, c0 + SPW)
                            S.op("dve", TT(U.t[:], Qb.t[:], U.t[:], ALU.mult), reads=[Qb.b, U.b], writes=[U.b])
                            if dr == 0:
                                init = 0.0 if sp == 0 else hf.t[:, c0 - 1:c0]
                                S.op("dve", lambda h: h.tensor_tensor_scan(
                                    out=hf.t[:, cols], data0=A.t[:], data1=U.t[:], initial=init, op0=ALU.mult, op1=ALU.add),
                                    reads=[A.b, U.b, hf.b], writes=[hf.b])
                            else:
                                first = (sp == NSP - 1)
                                hi_prev = hini[(sp + 1) % 2]
                                hi_cur = hini[sp % 2]
                                init = 0.0 if first else hi_prev.t[:, 0:1]
                                rd = [A.b, U.b] + ([] if first else [hi_prev.b])
                                S.op("dve", lambda h: h.tensor_tensor_scan(
                                    out=HB.t[:, ::-1], data0=A.t[:, ::-1], data1=U.t[:, ::-1], initial=init, op0=ALU.mult, op1=ALU.add),
                                    reads=rd, writes=[HB.b])
                                S.op("dve", CP(hi_cur.t[:, 0:1], HB.t[:, 0:1]), reads=[HB.b], writes=[hi_cur.b])
                                S.op("dve", STT(Zb.t[:], Zb.t[:], 1.0, GX.t[:], ALU.add, ALU.mult), reads=[Zb.b, GX.b], writes=[Zb.b])
                                S.op("dve", TT(HS.t[:], hf.t[:, cols], HB.t[:], ALU.add), reads=[hf.b, HB.b], writes=[HS.b])
                                yp = YB[cnt["yb"] % 2]
                                cnt["yb"] += 1
                                S.op("dve", TT(yp.t[:], HS.t[:], Zb.t[:], ALU.mult), reads=[HS.b, Zb.b], writes=[yp.b])
                                S.dma("sp", yb_d[:, c, cols], yp.t[:], reads=[yp.b], writes=[ybB[c]])

                        for k in range(len(jobs) + 1):
                            pend_conv = []
                            if k < len(jobs):
                                stage1a(k)
                                if k == 0 and NF > 1:
                                    pend_conv.append((c, jobs[1][1], jobs[1][2]))
                                if k + 2 < len(jobs) and jobs[k + 2][0] == 0:
                                    pend_conv.append((c, jobs[k + 2][1], jobs[k + 2][2]))
                                for pc_ in pend_conv:
                                    conv_mm(*pc_)
                            if k >= 1:
                                stage2a(k - 1)
                            if k < len(jobs):
                                stage1b(k)
                            if k >= 1:
                                stage2a_gelu(k - 1)
                            if k < len(jobs):
                                for pc_ in pend_conv:
                                    conv_ev(*pc_)
                                if k == NF and c + 1 < 10:
                                    xr_proj(c + 1, Wc[c + 1][0])
                                    conv_diag(c + 1)
                                if k == len(jobs) - 1 and c + 1 < 10:
                                    conv_sp(c + 1, 0, 0)
                            if k >= 1:
                                stage2b(k - 1)
                    S.barrier()

                with ExitStack() as stk:
                    alloc_wbufs(stk, 2048)
                    st = alloc_ac(stk)
                    yab_ = TB(sb(stk, "yag", [128, 4, G], BF16))
                    ybb_ = TB(sb(stk, "ybg", [128, 10, G], BF16))
                    mg = TB(sb(stk, "mg", [128, KC, G], BF16))
                    mgB = [Buf() for _ in range(KC)]
                    Tga = [TB(sb(stk, "Tga", [128, TG], F32)) for _ in range(2)]
                    Tgb = [TB(sb(stk, "Tgb", [128, TG], F32)) for _ in range(2)]
                    m1_ = TB(sb(stk, "m1", [128, TG], F32))
                    m2_ = TB(sb(stk, "m2", [128, TG], F32))
                    m1 = [m1_, m1_]
                    m2 = [m2_, m2_]
                    ci = 0
                    def c_loads(gj):
                        gsl_ = slice(gj * G, (gj + 1) * G)
                        S.dma("sp", yab_.t[:], ya_d[:, :, gsl_], reads=yaB, writes=[yab_.b])
                        S.dma("sp", ybb_.t[:], yb_d[:, :, gsl_], reads=ybB, writes=[ybb_.b])
                        S.dma("sp", st["xgs"][gj % 2][:], xres_d[:, :, gsl_], reads=[xresB[gj]], writes=st["xgBs"][gj % 2])

                    c_loads(0)
                    for gi in range(NG):
                        xg, xgB = st["xgs"][gi % 2], st["xgBs"][gi % 2]
                        gsl = slice(gi * G, (gi + 1) * G)
                        for m in range(8):
                            wpa_ = load_w(wb["wpa"][l, m], 512, ("wpa", l, m))
                            wpb_ = load_w(wb["wpb"][l, m], 1280, ("wpb", l, m))
                            wga_ = load_w(wb["win"][l, 56 + m], 1024, ("win", l, 56 + m))
                            wgb_ = load_w(wb["win"][l, 64 + m], 1024, ("win", l, 64 + m))
                            for t in range(G // TG):
                                cols = slice(t * TG, (t + 1) * TG)
                                gcols = slice(gi * G + t * TG, gi * G + (t + 1) * TG)
                                b_pa, b_pb, b_ga, b_gb = S.bank(), S.bank(), S.bank(), S.bank()
                                S.mm([MM(b_pa.t[:], wpa_.t[:, kc * 128:(kc + 1) * 128], yab_.t[:, kc, cols], kc == 0, kc == 3) for kc in range(4)],
                                     reads=[wpa_.b, yab_.b], writes=[b_pa.b])
                                S.mm([MM(b_pb.t[:], wpb_.t[:, kc * 128:(kc + 1) * 128], ybb_.t[:, kc, cols], kc == 0, kc == 9) for kc in range(10)],
                                     reads=[wpb_.b, ybb_.b], writes=[b_pb.b])
                                S.mm([MM(b_ga.t[:], wga_.t[:, kc * 128:(kc + 1) * 128], xn2[:, kc, gcols], kc == 0, kc == KC - 1) for kc in range(KC)],
                                     reads=[wga_.b, xn2B[gi]], writes=[b_ga.b])
                                S.mm([MM(b_gb.t[:], wgb_.t[:, kc * 128:(kc + 1) * 128], xn2[:, kc, gcols], kc == 0, kc == KC - 1) for kc in range(KC)],
                                     reads=[wgb_.b, xn2B[gi]], writes=[b_gb.b])
                                ta, tb_, a1, a2_ = Tga[ci % 2], Tgb[ci % 2], m1[ci % 2], m2[ci % 2]
                                ci += 1
                                S.op("act", ACT(ta.t[:], b_ga.t[:], AF.Tanh, bias=dpc("bin", l * 72 + 56 + m), scale=0.5), reads=[b_ga.b, dpt.b], writes=[ta.b])
                                S.op("act", ACT(tb_.t[:], b_gb.t[:], AF.Tanh, bias=dpc("bin", l * 72 + 64 + m), scale=0.5), reads=[b_gb.b, dpt.b], writes=[tb_.b])
                                S.op("dve", STT(a1.t[:], ta.t[:], 1.0, b_pa.t[:], ALU.add, ALU.mult), reads=[ta.b, b_pa.b], writes=[a1.b])
                                S.op("dve", STT(a2_.t[:], tb_.t[:], 1.0, b_pb.t[:], ALU.add, ALU.mult), reads=[tb_.b, b_pb.b], writes=[a2_.b])
                                S.op("dve", TT(mg.t[:, m, cols], a1.t[:], a2_.t[:], ALU.add), reads=[a1.b, a2_.b], writes=[mgB[m]])
                        pend = None
                        for mo in range(8):
                            wo_ = load_w(wb["wo"][l, mo], 1024, ("wo", l, mo))
                            for t in range(G // TG):
                                cols = slice(t * TG, (t + 1) * TG)
                                by = S.bank()
                                S.mm([MM(by.t[:], wo_.t[:, kc * 128:(kc + 1) * 128], mg.t[:, kc, cols], kc == 0, kc == KC - 1) for kc in range(KC)],
                                     reads=[wo_.b] + mgB, writes=[by.b])
                                S.op("dve", STT(xg[:, mo, cols], by.t[:], 0.5, xg[:, mo, cols], ALU.mult, ALU.add), reads=[by.b, xgB[mo]], writes=[xgB[mo]])
                                nsq = norm_sq(xg, xgB, mo, cols, st)
                                if pend is not None:
                                    norm_mm(*pend)
                                pend = nsq
                        norm_mm(*pend)
                        if gi + 1 < NG:
                            c_loads(gi + 1)
                        ffn(l, 1, xg, xgB, G, st, True)
                        if l + 1 < L:
                            phase_a(l + 1, gi, st, True)
                        else:
                            for t in range(G // TG):
                                cols = slice(t * TG, (t + 1) * TG)
                                norm_tile(xg, xgB, cols, "nf", 0, lambda kc: xg[:, kc, cols], lambda kc: xgB[kc], st, True)
                            for tt in range(G // 128):
                                io = st["io"][st["ioi"] % 2]
                                st["ioi"] += 1
                                for half in range(2):
                                    bank = S.bank()
                                    S.mm([TR(bank.t[:, j * 128:(j + 1) * 128], xg[:, half * 4 + j, tt * 128:(tt + 1) * 128], identf.t[:])
                                          for j in range(4)], reads=xgB + [identf.b], writes=[bank.b])
                                    S.op("act", ACT(io.t[:, half * 512:(half + 1) * 512], bank.t[:], AF.Copy), reads=[bank.b], writes=[io.b])
                                r0 = tok0 + gi * G + tt * 128
                                S.dma("pool", y_d[r0:r0 + 128, :], io.t[:], reads=[io.b], writes=[Buf()])
                    S.barrier()
        S.barrier()
    return nc


def _blk(w, kcn):
    K, N = w.shape
    return np.ascontiguousarray(w.reshape(kcn, 128, N // 128, 128).transpose(2, 1, 0, 3)).reshape(N // 128, 128, kcn * 128)


def prep_weights(inp, L):
    f32 = np.float32
    wgu = np.empty((L, 2, NFF, 128, 2 * KC * 128), f32)
    wd = np.empty((L, 2, 8, 2, 128, 11 * 128), f32)
    win = np.empty((L, 72, 128, KC * 128), f32)
    wrg = np.empty((L, 10, 128, 4 * 128), f32)
    wpa = np.empty((L, 8, 128, 4 * 128), f32)
    wpb = np.empty((L, 8, 128, 10 * 128), f32)
    wo = np.empty((L, 8, 128, 8 * 128), f32)
    for l in range(L):
        for f, pre in enumerate(("ffn1", "ffn2")):
            wgu[l, f, :, :, 0:1024] = _blk(np.asarray(inp[pre + "_w_gate"][l]), KC)
            wgu[l, f, :, :, 1024:2048] = _blk(np.asarray(inp[pre + "_w_up"][l]), KC)
            dwn = np.asarray(inp[pre + "_w_down"][l])
            for half in range(2):
                wd[l, f, :, half] = _blk(dwn[half * 1408:(half + 1) * 1408], 11)
        win[l] = _blk(np.asarray(inp["w_in"][l]), KC)
        wa = np.asarray(inp["rg_w_a"][l])
        wx = np.asarray(inp["rg_w_x"][l])
        for dr in range(2):
            wrg[l, :, :, (2 * dr) * 128:(2 * dr + 1) * 128] = wa[dr]
            wrg[l, :, :, (2 * dr + 1) * 128:(2 * dr + 2) * 128] = wx[dr]
        wpa[l] = _blk(np.asarray(inp["w_proj_a"][l]), 4)
        wpb[l] = _blk(np.asarray(inp["w_proj_b"][l]), 10)
        wo[l] = _blk(np.asarray(inp["w_out"][l]), 8)
    PO, NPP = pp_layout(L)
    pp = np.zeros((128, NPP), f32)

    def put(name, arr):
        a = np.asarray(arr, f32)
        a = a.reshape(-1, a.shape[-1] // 128, 128)
        a = a.transpose(2, 0, 1).reshape(128, -1)
        pp[:, PO[name]:PO[name] + a.shape[1]] = a

    put("n1", np.asarray(inp["ffn1_norm"])[:L])
    put("nm", np.asarray(inp["mix_norm"])[:L])
    put("n2", np.asarray(inp["ffn2_norm"])[:L])
    put("nf", np.asarray(inp["final_norm"])[None, :])
    put("bin", np.asarray(inp["b_in"])[:L])
    put("cw", np.asarray(inp["conv_w"])[:L].reshape(L * 4, 1280))
    put("cb", np.asarray(inp["conv_b"])[:L])
    put("ba", np.asarray(inp["rg_b_a"])[:L].reshape(L * 2, 1280))
    put("bx", np.asarray(inp["rg_b_x"])[:L].reshape(L * 2, 1280))
    put("lam", np.asarray(inp["rg_lambda"])[:L].reshape(L * 2, 1280))
    return {"wgu": wgu, "wd": wd, "win": win, "wrg": wrg, "wpa": wpa, "wpb": wpb, "wo": wo, "pp": pp,
            "b_in": np.ascontiguousarray(np.asarray(inp["b_in"], f32)[:L])}


def kernel(**inputs):
    xp = np.asarray(inputs["x_prompt"], np.float32)
    xs = np.asarray(inputs["x_sample"], np.float32)
    L = DEPTH
    B1, S1, _ = xp.shape
    B2, S2, _ = xs.shape
    n1 = B1 // NCORES
    n2 = B2 // NCORES
    seq_lens = [S1] * n1 + [S2] * n2
    wts = prep_weights(inputs, L)
    nc = build(seq_lens, L)
    in_maps = []
    for c in range(NCORES):
        xc = np.concatenate([xp[c * n1:(c + 1) * n1].reshape(-1, D), xs[c * n2:(c + 1) * n2].reshape(-1, D)], axis=0)
        m = {"x": np.ascontiguousarray(xc)}
        m.update(wts)
        in_maps.append(m)
    res = run_bass_kernel_spmd(nc, in_maps, core_ids=list(range(NCORES)))
    yp = np.empty_like(xp)
    ys = np.empty_like(xs)
    for c in range(NCORES):
        y = res.results[c]["y"]
        yp[c * n1:(c + 1) * n1] = y[:n1 * S1].reshape(n1, S1, D)
        ys[c * n2:(c + 1) * n2] = y[n1 * S1:].reshape(n2, S2, D)
    return (yp, ys)
```

```python
import numpy as np
from contextlib import ExitStack
import concourse.bass as bass
import concourse.mybir as mybir
from concourse.bass_utils import run_bass_kernel_spmd

F32 = mybir.dt.float32
BF16 = mybir.dt.bfloat16
I32 = mybir.dt.int32
AF = mybir.ActivationFunctionType
ALU = mybir.AluOpType

D = 1024
DFF = 2816
NFF = 22
KC = 8
INC = 9216
DEPTH = 4
NCORES = 8
GROUPS = ((128, 1), (512, 4), (2048, 16))
SLOPES = [2.0 ** (-8.0 * h / 12.0) for h in range(1, 13)]
EPS = 1e-6
GELU_C = 0.7978845608028654
TG = 512


def pp_layout(L):
    off = {}
    n = 0
    for name, cnt in (("n1", L * 8), ("nm", L * 8), ("n2", L * 8), ("nf", 8), ("bin", L * 72),
                      ("cw", L * 40), ("cb", L * 10), ("ba", L * 20), ("bx", L * 20), ("lam", L * 20)):
        off[name] = n
        n += cnt
    return off, n


class Buf:
    __slots__ = ("w", "r")

    def __init__(self):
        self.w = None
        self.r = {}


class TB:
    __slots__ = ("t", "b")

    def __init__(self, t, b=None):
        self.t = t
        self.b = b if b is not None else Buf()


class Eng:
    def __init__(self, name, h, key):
        self.name = name
        self.h = h
        self.key = key
        self.cnt = 0
        self.waited = {}
        self.dkeys = []
        self.dma_n = 0


class Sched:
    def __init__(self, nc):
        self.nc = nc
        self.sems = {}
        self.E = {}
        self.banks = []
        self.bank_i = 0

    def add_engine(self, name, h, sem, dsems=()):
        key = "e_" + name
        self.sems[key] = sem
        e = Eng(name, h, key)
        for i, s in enumerate(dsems):
            k = "d_%s_%d" % (name, i)
            self.sems[k] = s
            e.dkeys.append(k)
        self.E[name] = e

    def _need(self, e, tok):
        key, val = tok
        if e.waited.get(key, 0) >= val:
            return
        if key == e.key and e.name == "pe":
            return
        e.h.wait_ge(self.sems[key], val)
        e.waited[key] = val

    def _deps(self, e, reads, writes):
        for b in reads:
            if b.w is not None:
                self._need(e, b.w)
        for b in writes:
            if b.w is not None:
                self._need(e, b.w)
            for k, v in b.r.items():
                self._need(e, (k, v))

    def _commit(self, tok, reads, writes):
        k, v = tok
        for b in reads:
            if b.r.get(k, 0) < v:
                b.r[k] = v
        for b in writes:
            b.w = tok
            b.r = {}

    def op(self, en, fn, reads=(), writes=()):
        e = self.E[en]
        self._deps(e, reads, writes)
        ins = fn(e.h)
        e.cnt += 1
        ins.then_inc(self.sems[e.key], 1)
        tok = (e.key, e.cnt)
        self._commit(tok, reads, writes)
        return tok

    def mm(self, fns, reads=(), writes=()):
        e = self.E["pe"]
        self._deps(e, reads, writes)
        ins = None
        for f in fns:
            ins = f(e.h)
        e.cnt += 1
        ins.then_inc(self.sems[e.key], 1)
        tok = (e.key, e.cnt)
        self._commit(tok, reads, writes)
        return tok

    def dma(self, qn, out, in_, reads=(), writes=()):
        q = self.E[qn]
        self._deps(q, reads, writes)
        K = len(q.dkeys)
        k = q.dma_n % K
        rnd = q.dma_n // K
        key = q.dkeys[k]
        if rnd > 0:
            self._need(q, (key, 16 * rnd))
        ins = q.h.dma_start(out=out, in_=in_)
        ins.then_inc(self.sems[key], 16)
        q.dma_n += 1
        tok = (key, 16 * (rnd + 1))
        self._commit(tok, reads, writes)
        return tok

    def barrier(self):
        toks = []
        for e in self.E.values():
            if e.cnt > 0:
                toks.append((e.key, e.cnt))
            K = len(e.dkeys)
            for k in range(K):
                n = (e.dma_n - k + K - 1) // K
                if n > 0:
                    toks.append((e.dkeys[k], 16 * n))
        for e in self.E.values():
            for t in toks:
                self._need(e, t)

    def bank(self):
        b = self.banks[self.bank_i % len(self.banks)]
        self.bank_i += 1
        return b


def MM(out, lhsT, rhs, start, stop):
    return lambda h: h.matmul(out, lhsT, rhs, start=start, stop=stop)


def TR(out, in_, ident):
    return lambda h: h.transpose(out, in_, ident)


def ACT(out, in_, func, bias=None, scale=None):
    kw = {}
    if bias is not None:
        kw["bias"] = bias
    if scale is not None:
        kw["scale"] = scale
    return lambda h: h.activation(out=out, in_=in_, func=func, **kw)


def STT(out, in0, scalar, in1, op0, op1):
    return lambda h: h.scalar_tensor_tensor(out=out, in0=in0, scalar=scalar, in1=in1, op0=op0, op1=op1)


def TT(out, in0, in1, op):
    return lambda h: h.tensor_tensor(out=out, in0=in0, in1=in1, op=op)


def TS(out, in0, s1, s2, op0, op1=None):
    if op1 is None:
        return lambda h: h.tensor_scalar(out=out, in0=in0, scalar1=s1, scalar2=None, op0=op0)
    return lambda h: h.tensor_scalar(out=out, in0=in0, scalar1=s1, scalar2=s2, op0=op0, op1=op1)


def CP(out, in_):
    return lambda h: h.tensor_copy(out=out, in_=in_)


def build(seq_lens, L, dbg=None):
    NTOK = sum(seq_lens)
    SMAXU = 4096
    SMAX = max(max(seq_lens), min(SMAXU, sum(seq_lens)))
    PO, NPP = pp_layout(L)
    nc = bass.Bass("TRN2", target_bir_lowering=False)
    x_d = nc.dram_tensor("x", [NTOK, D], F32, kind="ExternalInput").ap()
    y_d = nc.dram_tensor("y", [NTOK, D], F32, kind="ExternalOutput").ap()
    wshape = {
        "wgu": [L, 2, NFF, 128, 2 * KC * 128],
        "wd": [L, 2, 8, 2, 128, 11 * 128],
        "win": [L, 72, 128, KC * 128],
        "wrg": [L, 10, 128, 4 * 128],
        "wpa": [L, 8, 128, 4 * 128],
        "wpb": [L, 8, 128, 10 * 128],
        "wo": [L, 8, 128, 8 * 128],
    }
    wf = {k: nc.dram_tensor(k, s, F32, kind="ExternalInput").ap() for k, s in wshape.items()}
    wb = {k: nc.dram_tensor(k + "_b", s, BF16, kind="Internal").ap() for k, s in wshape.items()}
    pp_d = nc.dram_tensor("pp", [128, NPP], F32, kind="ExternalInput").ap()
    bin_d = nc.dram_tensor("b_in", [L, INC], F32, kind="ExternalInput").ap()
    xres_d = nc.dram_tensor("xres", [128, 8, SMAX], F32, kind="Internal").ap()
    ya_d = nc.dram_tensor("ya_s", [128, 4, SMAX], BF16, kind="Internal").ap()
    yb_d = nc.dram_tensor("yb_s", [128, 10, SMAX], BF16, kind="Internal").ap()
    e_d = nc.dram_tensor("e_s", [128, 12, 256], BF16, kind="Internal").ap()
    dbg_d = {}
    if dbg:
        for name, shp in dbg.items():
            dbg_d[name] = nc.dram_tensor("dbg_" + name, shp, F32, kind="ExternalOutput").ap()

    uid = [0]

    def sb(stack, name, shape, dt):
        uid[0] += 1
        return stack.enter_context(nc.sbuf_tensor("%s_%d" % (name, uid[0]), shape, dt))

    with ExitStack() as top:
        S = Sched(nc)
        nsem = [0]

        def newsem():
            nsem[0] += 1
            return top.enter_context(nc.semaphore("s%d" % nsem[0]))

        S.add_engine("pe", nc.tensor, newsem())
        S.add_engine("act", nc.scalar, newsem())
        S.add_engine("dve", nc.vector, newsem())
        S.add_engine("pool", nc.gpsimd, newsem(), [newsem() for _ in range(12)])
        S.add_engine("sp", nc.sync, newsem(), [newsem() for _ in range(12)])
        for i in range(7):
            S.banks.append(TB(top.enter_context(nc.psum_tensor("ps%d" % i, [128, 512], F32))))
        statbank = TB(top.enter_context(nc.psum_tensor("ps_stat", [128, 512], F32)))

        ppt = TB(sb(top, "pp", [128, NPP], F32))
        dpt = TB(sb(top, "dp", [128, NPP], F32))
        identf = TB(sb(top, "identf", [128, 128], F32))
        onesb = TB(sb(top, "onesb", [128, 128], BF16))
        eB = Buf()
        xn2 = sb(top, "xn2", [128, KC, SMAX], BF16)
        WBN = 6
        wbufs = []
        wbi = [0]

        def alloc_wbufs(stk, ncols):
            wbufs[:] = [TB(sb(stk, "wbuf%d" % i, [128, ncols], BF16)) for i in range(WBN)]

        def ppc(name, idx):
            return ppt.t[:, PO[name] + idx: PO[name] + idx + 1]

        def dpc(name, idx):
            return dpt.t[:, PO[name] + idx: PO[name] + idx + 1]

        S.dma("sp", ppt.t[:], pp_d[:, :], writes=[ppt.b])

        wB = {}

        def cast_jobs(l):
            jobs = []
            for f in range(2):
                for m in range(NFF):
                    jobs.append((wb["wgu"][l, f, m], wf["wgu"][l, f, m], [("wgu", l, f, m)]))
                for mo in range(8):
                    jobs.append((wb["wd"][l, f, mo], wf["wd"][l, f, mo], [("wd", l, f, mo)]))
            for m4 in range(18):
                jobs.append((wb["win"][l, 4 * m4:4 * m4 + 4], wf["win"][l, 4 * m4:4 * m4 + 4], [("win", l, 4 * m4 + j) for j in range(4)]))
            for key, nsplit in (("wrg", 1), ("wpa", 1), ("wpb", 2), ("wo", 2)):
                n = wshape[key][1]
                step = n // nsplit
                for s0 in range(0, n, step):
                    jobs.append((wb[key][l, s0:s0 + step], wf[key][l, s0:s0 + step], [(key, l, j) for j in range(s0, s0 + step)]))
            return jobs

        def issue_casts(jobs):
            for (o, i, keys) in jobs:
                b = Buf()
                S.dma("pool", o, i, writes=[b])
                for kk in keys:
                    wB[kk] = b

        def cast_layer(l):
            issue_casts(cast_jobs(l))

        cast_layer(0)

        def load_w(src_ap, ncols, bkey):
            w = wbufs[wbi[0] % WBN]
            wbi[0] += 1
            S.dma("sp", w.t[:, 0:ncols], src_ap, reads=[wB[bkey]], writes=[w.b])
            return w

        with ExitStack() as st0:
            idi = TB(sb(st0, "idi", [128, 256], I32))
            relf = TB(sb(st0, "relf", [128, 256], F32))
            absr = TB(sb(st0, "absr", [128, 256], F32))
            msk = TB(sb(st0, "msk", [128, 256], F32))
            tmpE = TB(sb(st0, "tmpE", [128, 256], F32))
            Et = TB(sb(st0, "E", [128, 12, 256], BF16))
            S.op("pool", lambda h: h.iota(idi.t[:, 0:128], pattern=[[-1, 128]], base=0, channel_multiplier=1),
                 writes=[idi.b])
            S.op("dve", CP(relf.t[:, 0:128], idi.t[:, 0:128]), reads=[idi.b], writes=[relf.b])
            S.op("dve", TS(identf.t[:], relf.t[:, 0:128], 0.0, None, ALU.is_equal), reads=[relf.b], writes=[identf.b])
            S.op("dve", lambda h: h.memset(onesb.t[:], 1.0), writes=[onesb.b])
            S.op("pool", lambda h: h.iota(idi.t[:], pattern=[[-1, 256]], base=64, channel_multiplier=1),
                 reads=[relf.b], writes=[idi.b])
            S.op("dve", CP(relf.t[:], idi.t[:]), reads=[idi.b], writes=[relf.b])
            S.op("act", ACT(absr.t[:], relf.t[:], AF.Abs), reads=[relf.b], writes=[absr.b])
            S.op("dve", TS(msk.t[:], absr.t[:], 64.5, None, ALU.is_le), reads=[absr.b], writes=[msk.b])
            for hh in range(12):
                d = GROUPS[hh // 4][1]
                S.op("act", ACT(tmpE.t[:], absr.t[:], AF.Exp, scale=-SLOPES[hh] * d), reads=[absr.b], writes=[tmpE.b])
                S.op("dve", TT(Et.t[:, hh, :], tmpE.t[:], msk.t[:], ALU.mult), reads=[tmpE.b, msk.b], writes=[Et.b])
            S.dma("sp", e_d[:, :, :], Et.t[:], reads=[Et.b], writes=[eB])
            S.op("dve", CP(dpt.t[:], ppt.t[:]), reads=[ppt.b], writes=[dpt.b])
            for name, cnt in (("bin", L * 72), ("ba", L * 20), ("bx", L * 20)):
                o = PO[name]
                S.op("dve", TS(dpt.t[:, o:o + cnt], ppt.t[:, o:o + cnt], 0.5, None, ALU.mult), reads=[ppt.b], writes=[dpt.b])
            o = PO["lam"]
            cnt = L * 20
            lt = TB(sb(st0, "lt", [128, cnt], F32))
            S.op("act", ACT(lt.t[:], ppt.t[:, o:o + cnt], AF.Exp, scale=-1.0), reads=[ppt.b], writes=[lt.b])
            S.op("act", ACT(lt.t[:], lt.t[:], AF.Ln, bias=1.0), reads=[lt.b], writes=[lt.b])
            S.op("dve", TS(dpt.t[:, o:o + cnt], lt.t[:], -4.0, None, ALU.mult), reads=[lt.b], writes=[dpt.b])
            S.barrier()

        def norm_sq(xg, xgB, kc, cols, st):
            sq = st["sq"][st["sqi"] % 3]
            st["sqi"] += 1
            S.op("act", ACT(sq.t[:], xg[:, kc, cols], AF.Square), reads=[xgB[kc]], writes=[sq.b])
            return (sq, kc)

        def norm_mm(sq, kc):
            S.mm([MM(statbank.t[:], onesb.t[:], sq.t[:], kc == 0, kc == KC - 1)], reads=[sq.b, onesb.b], writes=[statbank.b])

        def norm_stat(xg, xgB, kc, cols, st):
            norm_mm(*norm_sq(xg, xgB, kc, cols, st))

        def norm_fin(xg, xgB, cols, gname, gidx0, out_fn, outB, st):
            sd = st["sd"]
            S.op("act", ACT(sd.t[:], statbank.t[:], AF.Sqrt, bias=st["epsb"].t[:, 0:1], scale=1.0 / D), reads=[statbank.b, st["epsb"].b], writes=[sd.b])
            S.op("dve", lambda h: h.reciprocal(out=sd.t[:], in_=sd.t[:]), reads=[sd.b], writes=[sd.b])
            for kc in range(KC):
                S.op("dve", STT(out_fn(kc), xg[:, kc, cols], ppc(gname, gidx0 + kc), sd.t[:], ALU.mult, ALU.mult),
                     reads=[xgB[kc], sd.b, ppt.b], writes=[outB(kc) if callable(outB) else outB])

        def norm_tile(xg, xgB, cols, gname, gidx0, out_fn, outB, st, stats_done=False):
            if not stats_done:
                for kc in range(KC):
                    norm_stat(xg, xgB, kc, cols, st)
            norm_fin(xg, xgB, cols, gname, gidx0, out_fn, outB, st)

        def ffn(l, f, xg, xgB, G, st, stats_done=False):
            nt = G // TG
            assert nt == 1
            xn = st["xn"]
            for t in range(nt):
                cols = slice(t * TG, (t + 1) * TG)
                norm_tile(xg, xgB, cols, "n1" if f == 0 else "n2", l * 8, lambda kc: xn.t[:, kc, cols], lambda kc: st["xnB"][kc], st, stats_done)
            h = st["h"]
            for half in range(2):
                for mi in range(11):
                    m = half * 11 + mi
                    w = load_w(wb["wgu"][l, f, m], 2048, ("wgu", l, f, m))
                    for t in range(nt):
                        cols = slice(t * TG, (t + 1) * TG)
                        bg = S.bank()
                        bu = S.bank()
                        if m == 0:
                            for kc in range(KC):
                                S.mm([MM(bg.t[:], w.t[:, kc * 128:(kc + 1) * 128], xn.t[:, kc, cols], kc == 0, kc == KC - 1)],
                                     reads=[w.b, st["xnB"][kc]], writes=[bg.b])
                                S.mm([MM(bu.t[:], w.t[:, (8 + kc) * 128:(9 + kc) * 128], xn.t[:, kc, cols], kc == 0, kc == KC - 1)],
                                     reads=[w.b, st["xnB"][kc]], writes=[bu.b])
                        else:
                            S.mm([MM(bg.t[:], w.t[:, kc * 128:(kc + 1) * 128], xn.t[:, kc, cols], kc == 0, kc == KC - 1) for kc in range(KC)],
                                 reads=[w.b] + st["xnB"], writes=[bg.b])
                            S.mm([MM(bu.t[:], w.t[:, (8 + kc) * 128:(9 + kc) * 128], xn.t[:, kc, cols], kc == 0, kc == KC - 1) for kc in range(KC)],
                                 reads=[w.b] + st["xnB"], writes=[bu.b])
                        T = st["T"][st["Ti"] % 2]
                        W = st["W"][st["Ti"] % 2]
                        st["Ti"] += 1
                        S.op("act", ACT(T.t[:], bg.t[:], AF.Tanh, scale=0.5), reads=[bg.b], writes=[T.b])
                        S.op("dve", STT(W.t[:], T.t[:], 1.0, bg.t[:], ALU.add, ALU.mult), reads=[T.b, bg.b], writes=[W.b])
                        S.op("dve", TT(h.t[:, mi, cols], W.t[:], bu.t[:], ALU.mult), reads=[W.b, bu.b], writes=[st["hB"][mi]])
                pend = None
                for mo in range(8):
                    w = load_w(wb["wd"][l, f, mo, half], 1408, ("wd", l, f, mo))
                    for t in range(nt):
                        cols = slice(t * TG, (t + 1) * TG)
                        by = S.bank()
                        S.mm([MM(by.t[:], w.t[:, kc * 128:(kc + 1) * 128], h.t[:, kc, cols], kc == 0, kc == 10) for kc in range(11)],
                             reads=[w.b] + st["hB"], writes=[by.b])
                        S.op("dve", STT(xg[:, mo, cols], by.t[:], 0.25, xg[:, mo, cols], ALU.mult, ALU.add),
                             reads=[by.b, xgB[mo]], writes=[xgB[mo]])
                        if half == 1:
                            nsq = norm_sq(xg, xgB, mo, cols, st)
                            if pend is not None:
                                norm_mm(*pend)
                            pend = nsq
                if pend is not None:
                    norm_mm(*pend)

        units = []
        i_ = 0
        while i_ < len(seq_lens):
            nb_ = 1
            while (i_ + nb_ < len(seq_lens) and seq_lens[i_ + nb_] == seq_lens[i_] and (nb_ + 1) * seq_lens[i_] <= SMAXU):
                nb_ += 1
            units.append((sum(seq_lens[:i_]), seq_lens[i_], nb_))
            i_ += nb_
        for si, (tok0, SL, NB) in enumerate(units):
            ST = NB * SL
            G = TG
            NG = ST // G
            xn2B = [Buf() for _ in range(NG)]
            xresB = [Buf() for _ in range(NG)]
            yaB = [Buf() for _ in range(4)]
            ybB = [Buf() for _ in range(10)]

            def alloc_ac(stk):
                st = {}
                st["xgs"] = [sb(stk, "xg", [128, KC, G], F32) for _ in range(2)]
                st["xgBs"] = [[Buf() for _ in range(KC)] for _ in range(2)]
                st["xn"] = TB(sb(stk, "xn", [128, KC, G], BF16))
                st["xnB"] = [Buf() for _ in range(KC)]
                st["h"] = TB(sb(stk, "h", [128, 11, G], BF16))
                st["hB"] = [Buf() for _ in range(11)]
                st["sq"] = [TB(sb(stk, "sq", [128, TG], BF16)) for _ in range(3)]
                st["sqi"] = 0
                st["sd"] = TB(sb(stk, "sd", [128, TG], F32))
                st["T"] = [TB(sb(stk, "T", [128, TG], F32)) for _ in range(2)]
                st["W"] = [TB(sb(stk, "W", [128, TG], F32)) for _ in range(2)]
                st["Ti"] = 0
                st["epsb"] = TB(sb(stk, "epsb", [128, 1], F32))
                S.op("dve", lambda h: h.memset(st["epsb"].t[:], EPS), writes=[st["epsb"].b])
                one_io = TB(sb(stk, "io", [128, D], F32))
                st["io"] = [one_io, one_io]
                st["ioi"] = 0
                return st

            def phase_a(l, gi, st, stats_done=False, store_q="pool"):
                xg, xgB = st["xgs"][gi % 2], st["xgBs"][gi % 2]
                ffn(l, 0, xg, xgB, G, st, stats_done)
                for t in range(G // TG):
                    cols = slice(t * TG, (t + 1) * TG)
                    gcols = slice(gi * G + t * TG, gi * G + (t + 1) * TG)
                    norm_tile(xg, xgB, cols, "nm", l * 8, lambda kc: xn2[:, kc, gcols], xn2B[gi], st, True)
                S.dma(store_q, xres_d[:, :, gi * G:(gi + 1) * G], xg[:], reads=xgB, writes=[xresB[gi]])

            with ExitStack() as stk:
                alloc_wbufs(stk, 2048)
                st = alloc_ac(stk)
                for gi in range(NG):
                    xg, xgB = st["xgs"][gi % 2], st["xgBs"][gi % 2]
                    for tt in range(G // 128):
                        io = st["io"][st["ioi"] % 2]
                        st["ioi"] += 1
                        r0 = tok0 + gi * G + tt * 128
                        S.dma("sp", io.t[:], x_d[r0:r0 + 128, :], writes=[io.b])
                        for half in range(2):
                            bank = S.bank()
                            S.mm([TR(bank.t[:, j * 128:(j + 1) * 128], io.t[:, (half * 4 + j) * 128:(half * 4 + j + 1) * 128], identf.t[:])
                                  for j in range(4)], reads=[io.b, identf.b], writes=[bank.b])
                            for j in range(4):
                                kc = half * 4 + j
                                S.op("act", ACT(xg[:, kc, tt * 128:(tt + 1) * 128], bank.t[:, j * 128:(j + 1) * 128], AF.Copy),
                                     reads=[bank.b], writes=[xgB[kc]])
                    phase_a(0, gi, st, False, "sp" if si == 0 else "pool")
                S.barrier()

            for l in range(L):
                with ExitStack() as stk:
                    alloc_wbufs(stk, 1024)
                    bvbc = TB(sb(stk, "bvbc", [128, 1536], F32))
                    Et = TB(sb(stk, "E", [128, 12, 256], BF16))
                    S.dma("sp", Et.t[:], e_d[:, :, :], reads=[eB], writes=[Et.b])
                    S.dma("sp", bvbc.t[:], bin_d[l:l + 1, 3072:4608].partition_broadcast(128), writes=[bvbc.b])
                    qT = [TB(sb(stk, "qT", [128, ST], BF16)) for _ in range(2)]
                    kT = [TB(sb(stk, "kT", [128, ST], BF16)) for _ in range(2)]
                    Vt = [TB(sb(stk, "Vt", [128, ST // 128, 128], BF16)) for _ in range(2)]
                    acc = TB(sb(stk, "acc", [128, 2, ST], F32))
                    Xb = [TB(sb(stk, "Xb", [128, 256], F32)) for _ in range(4)]
                    Pb = [TB(sb(stk, "Pb", [128, 256], BF16)) for _ in range(10)]
                    accB = [Buf() for _ in range(4)]
                    fz = TB(sb(stk, "fz", [128, 2], F32))
                    yab = TB(sb(stk, "yab", [128, ST], BF16))
                    heads = [(hs, g) for hs in range(4) for g in range(3)]

                    def load_head(hs, g):
                        hh = 4 * g + hs
                        return [load_w(wb["win"][l, mm_], 1024, ("win", l, mm_)) for mm_ in (hh, 12 + hh, 24 + hh)]

                    nxt = load_head(*heads[0])
                    for hi, (hs, g) in enumerate(heads):
                        hh = 4 * g + hs
                        d = GROUPS[g][1]
                        Lr = SL // d
                        nkt = Lr // 128
                        wq, wk, wv = nxt
                        if hi + 1 < len(heads):
                            nxt = load_head(*heads[hi + 1])
                        q = qT[hi % 2]
                        k = kT[hi % 2]
                        V = Vt[hi % 2]
                        qvs = [q.t[:, b_ * SL:(b_ + 1) * SL].rearrange("p (r m) -> p r m", r=d) for b_ in range(NB)]
                        kvs = [k.t[:, b_ * SL:(b_ + 1) * SL].rearrange("p (r m) -> p r m", r=d) for b_ in range(NB)]
                        for (wt, dsts, dstB, mcol) in ((wq, qvs, q.b, hh), (wk, kvs, k.b, 12 + hh)):
                            for ti in range(ST // TG):
                                bank = S.bank()
                                S.mm([MM(bank.t[:], wt.t[:, kc * 128:(kc + 1) * 128], xn2[:, kc, ti * TG:(ti + 1) * TG], kc == 0, kc == KC - 1)
                                      for kc in range(KC)], reads=[wt.b, xn2B[ti * TG // G]], writes=[bank.b])
                                b_ = (ti * TG) // SL
                                m0 = (ti * TG - b_ * SL) // d
                                S.op("act", ACT(dsts[b_][:, :, m0:m0 + TG // d], bank.t[:].rearrange("p (m r) -> p r m", r=d), AF.Identity,
                                                bias=ppc("bin", l * 72 + mcol)), reads=[bank.b, ppt.b], writes=[dstB])
                        ntile = d * nkt
                        for b_ in range(NB):
                            xv = xn2[:, :, b_ * SL:(b_ + 1) * SL].rearrange("p c (m r) -> p c r m", r=d)
                            for t4 in range(0, ntile, 4):
                                bank = S.bank()
                                fns = []
                                for j in range(4):
                                    r, kt = divmod(t4 + j, nkt)
                                    for kc in range(KC):
                                        fns.append(MM(bank.t[:, j * 128:(j + 1) * 128], xv[:, kc, r, kt * 128:(kt + 1) * 128],
                                                      wv.t[:, kc * 128:(kc + 1) * 128], kc == 0, kc == KC - 1))
                                S.mm(fns, reads=[wv.b] + xn2B, writes=[bank.b])
                                S.op("dve", TT(V.t[:, b_ * ntile + t4:b_ * ntile + t4 + 4, :], bank.t[:].rearrange("p (j c) -> p j c", j=4),
                                               bvbc.t[:, hh * 128:(hh + 1) * 128].unsqueeze(1).to_broadcast([128, 4, 128]), ALU.add),
                                     reads=[bank.b, bvbc.b], writes=[V.b])
                        accvs = [acc.t[:, :, b_ * SL:(b_ + 1) * SL].rearrange("p u (m r) -> p u r m", r=d) for b_ in range(NB)]
                        S.op("dve", lambda h: h.memset(fz.t[:, 0:1], 0.0), reads=accB, writes=accB + [fz.b])
                        its = [(b_, r, kt) for b_ in range(NB) for r in range(d) for kt in range(nkt + 1)]
                        Pof = {}

                        def stage1(idx):
                            b_, r, kt = its[idx]
                            kv = kvs[b_]
                            qv = qvs[b_]
                            if kt >= nkt:
                                return
                            qs = max(0, 128 * kt - 64)
                            qe = min(Lr, 128 * kt + 192)
                            c0 = qs - (128 * kt - 64)
                            ncol = qe - qs
                            bank = S.bank()
                            S.mm([MM(bank.t[:, 0:ncol], kv[:, r, kt * 128:(kt + 1) * 128], qv[:, r, qs:qe], True, True)],
                                 reads=[k.b, q.b], writes=[bank.b])
                            X = Xb[idx % len(Xb)]
                            P = Pb[idx % len(Pb)]
                            S.op("act", ACT(X.t[:, 0:ncol], bank.t[:, 0:ncol], AF.Exp, scale=128.0 ** -0.5), reads=[bank.b], writes=[X.b])
                            S.op("pool", TT(P.t[:, c0:c0 + ncol], X.t[:, 0:ncol], Et.t[:, hh, c0:c0 + ncol], ALU.mult),
                                 reads=[X.b, Et.b], writes=[P.b])
                            Pof[(b_, r, kt)] = P

                        def stage2(idx):
                            b_, r, kt = its[idx]
                            accv = accvs[b_]
                            j = kt
                            a = max(0, 128 * j - 64)
                            bnd = min(Lr, 128 * j + 64)
                            n = bnd - a
                            srcs = [(Pof[(b_, r, kk)], kk) for kk in (kt - 1, kt) if 0 <= kk < nkt]
                            bank = S.bank()
                            fns = []
                            for which in range(2):
                                for si_, (Pm, ktp) in enumerate(srcs):
                                    cc = a - (128 * ktp - 64)
                                    lhs = V.t[:, b_ * ntile + r * nkt + ktp, :] if which == 0 else onesb.t[:]
                                    fns.append(MM(bank.t[:, which * 128:which * 128 + n], lhs, Pm.t[:, cc:cc + n], si_ == 0, si_ == len(srcs) - 1))
                            S.mm(fns, reads=[V.b, onesb.b] + [s_[0].b for s_ in srcs], writes=[bank.b])
                            src = bank.t[:, 0:256].rearrange("p (u c) -> p u c", u=2)[:, :, 0:n]
                            dst = accv[:, :, r, a:bnd]
                            ab = accB[idx % 4]
                            if g == 0:
                                S.op("act", ACT(dst, src, AF.Copy), reads=[bank.b], writes=[ab])
                            else:
                                S.op("dve", TT(dst, src, dst, ALU.add), reads=[bank.b, ab], writes=[ab])

                        LEAD = 5
                        for idx in range(len(its) + LEAD):
                            if idx < len(its):
                                stage1(idx)
                            if idx >= LEAD:
                                stage2(idx - LEAD)
                        if g == 2:
                            S.op("dve", lambda h: h.reciprocal(out=acc.t[:, 1, :], in_=acc.t[:, 1, :]), reads=accB, writes=accB)
                            S.op("dve", TT(yab.t[:], acc.t[:, 0, :], acc.t[:, 1, :], ALU.mult), reads=accB, writes=[yab.b])
                            S.dma("sp", ya_d[:, hs, 0:ST], yab.t[:], reads=[yab.b], writes=[yaB[hs]])
                    S.barrier()

                with ExitStack() as stk:
                    alloc_wbufs(stk, 1024)
                    SPW = 1024
                    NSP = SL // SPW
                    NF = NB * NSP
                    PPS = SPW // TG
                    xr = TB(sb(stk, "xr", [128, NB * (SL + 4)], F32))
                    xc = sb(stk, "xc", [128, ST], F32)
                    xcb = sb(stk, "xcb", [128, ST], BF16)
                    xcB = [Buf() for _ in range(NF)]
                    xcbB = [Buf() for _ in range(NF)]
                    hf = TB(sb(stk, "hf", [128, ST], F32))
                    Ab = [TB(sb(stk, "A", [128, SPW], F32)) for _ in range(2)]
                    Ub = [TB(sb(stk, "U", [128, SPW], F32)) for _ in range(2)]
                    Qbs = [TB(sb(stk, "Q", [128, SPW], F32)) for _ in range(2)]
                    HB = TB(sb(stk, "HB", [128, SPW], F32))
                    HS = TB(sb(stk, "HS", [128, SPW], F32))
                    hini = [TB(sb(stk, "hini", [128, 1], F32)) for _ in range(2)]
                    GXb = [TB(sb(stk, "GX", [128, SPW], F32)) for _ in range(2)]
                    Zbb = [TB(sb(stk, "Z", [128, SPW], F32)) for _ in range(2)]
                    YB = [TB(sb(stk, "YB", [128, SPW], BF16)) for _ in range(2)]
                    Trb = [TB(sb(stk, "Tr", [128, TG], F32)) for _ in range(2)]
                    Tib = [TB(sb(stk, "Ti", [128, TG], F32)) for _ in range(2)]
                    cnt = {"pc": 0, "yb": 0, "job": 0}
                    S.op("pool", lambda h: h.memset(xr.t[:], 0.0), writes=[xr.b])
                    qb = TB(sb(stk, "qb", [128, 1], F32))
                    S.op("dve", lambda h: h.memset(qb.t[:], 0.0625), writes=[qb.b])

                    def load_chunk(c):
                        return [load_w(wb["win"][l, 36 + c], 1024, ("win", l, 36 + c)),
                                load_w(wb["win"][l, 46 + c], 1024, ("win", l, 46 + c)),
                                load_w(wb["wrg"][l, c], 512, ("wrg", l, c))]

                    def xr_proj(c, wxr):
                        for ti in range(ST // TG):
                            bank = S.bank()
                            S.mm([MM(bank.t[:], wxr.t[:, kc * 128:(kc + 1) * 128], xn2[:, kc, ti * TG:(ti + 1) * TG], kc == 0, kc == KC - 1)
                                  for kc in range(KC)], reads=[wxr.b, xn2B[ti * TG // G]], writes=[bank.b])
                            b_ = (ti * TG) // SL
                            xo = b_ * (SL + 4) + 2 + (ti * TG - b_ * SL)
                            S.op("act", ACT(xr.t[:, xo:xo + TG], bank.t[:], AF.Identity, bias=ppc("bin", l * 72 + 36 + c)),
                                 reads=[bank.b, ppt.b], writes=[xr.b])

                    dgs = [TB(sb(stk, "dg", [128, 4, 128], F32)) for _ in range(2)]

                    def conv_diag(c):
                        dg = dgs[c % 2]
                        for jt in range(4):
                            S.op("pool", TS(dg.t[:, jt, :], identf.t[:], ppc("cw", l * 40 + jt * 10 + c), None, ALU.mult),
                                 reads=[identf.b, ppt.b], writes=[dg.b])

                    conv_banks = {}

                    def conv_mm(c, b_, sp):
                        dg = dgs[c % 2]
                        c0 = b_ * (SL + 4) + sp * SPW
                        bl = []
                        for pp_ in range(PPS):
                            t0 = c0 + pp_ * TG
                            bank = S.bank()
                            S.mm([MM(bank.t[:], dg.t[:, jt, :], xr.t[:, t0 + jt:t0 + jt + TG], jt == 0, jt == 3) for jt in range(4)],
                                 reads=[dg.b, xr.b], writes=[bank.b])
                            bl.append(bank)
                        conv_banks[(c, b_, sp)] = bl

                    def conv_ev(c, b_, sp):
                        c0 = b_ * SL + sp * SPW
                        gsp = b_ * NSP + sp
                        for pp_, bank in enumerate(conv_banks.pop((c, b_, sp))):
                            t0 = c0 + pp_ * TG
                            S.op("act", ACT(xc[:, t0:t0 + TG], bank.t[:], AF.Identity, bias=ppc("cb", l * 10 + c)), reads=[bank.b, ppt.b], writes=[xcB[gsp]])
                        S.op("pool", CP(xcb[:, c0:c0 + SPW], xc[:, c0:c0 + SPW]), reads=[xcB[gsp]], writes=[xcbB[gsp]])

                    def conv_sp(c, b_, sp):
                        conv_mm(c, b_, sp)
                        conv_ev(c, b_, sp)

                    Wc = [load_chunk(0)]
                    xr_proj(0, Wc[0][0])
                    conv_diag(0)
                    conv_sp(0, 0, 0)
                    pend_casts = cast_jobs(l + 1) if (si == 0 and l + 1 < L) else []
                    for c in range(10):
                        wxr, wgr, wrg = Wc[c]
                        if c + 1 < 10:
                            Wc.append(load_chunk(c + 1))
                        if pend_casts and c % 2 == 0:
                            npart = (len(pend_casts) + (4 - c // 2)) // (5 - c // 2)
                            issue_casts(pend_casts[:npart])
                            pend_casts = pend_casts[npart:]

                        jobs = ([(0, b_, sp) for b_ in range(NB) for sp in range(NSP)]
                                + [(1, b_, sp) for b_ in range(NB) for sp in range(NSP - 1, -1, -1)])
                        jst = {}

                        def stage1a(k):
                            dr, b_, sp = jobs[k]
                            c0 = b_ * SL + sp * SPW
                            gsp = b_ * NSP + sp
                            j = cnt["job"]
                            cnt["job"] += 1
                            A = Ab[j % 2]
                            U = Ub[j % 2]
                            Qb = Qbs[j % 2]
                            pidx = l * 20 + dr * 10 + c
                            trs = []
                            for pp_ in range(PPS):
                                cols = slice(c0 + pp_ * TG, c0 + (pp_ + 1) * TG)
                                Tr = Trb[cnt["pc"] % 2]
                                Ti = Tib[cnt["pc"] % 2]
                                cnt["pc"] += 1
                                ba_ = S.bank()
                                bx_ = S.bank()
                                S.mm([MM(ba_.t[:], wrg.t[:, (2 * dr) * 128:(2 * dr + 1) * 128], xcb[:, cols], True, True)], reads=[wrg.b, xcbB[gsp]], writes=[ba_.b])
                                S.mm([MM(bx_.t[:], wrg.t[:, (2 * dr + 1) * 128:(2 * dr + 2) * 128], xcb[:, cols], True, True)], reads=[wrg.b, xcbB[gsp]], writes=[bx_.b])
                                S.op("act", ACT(Tr.t[:], ba_.t[:], AF.Tanh, bias=dpc("ba", pidx), scale=0.5), reads=[ba_.b, dpt.b], writes=[Tr.b])
                                S.op("act", ACT(Ti.t[:], bx_.t[:], AF.Tanh, bias=dpc("bx", pidx), scale=0.5), reads=[bx_.b, dpt.b], writes=[Ti.b])
                                trs.append((Tr, Ti))
                            GX = Zb = None
                            if dr == 1:
                                GX = GXb[j % 2]
                                Zb = Zbb[j % 2]
                                for pp_ in range(PPS):
                                    pcols = slice(c0 + pp_ * TG, c0 + (pp_ + 1) * TG)
                                    lc = slice(pp_ * TG, (pp_ + 1) * TG)
                                    bank = S.bank()
                                    S.mm([MM(bank.t[:], wgr.t[:, kc * 128:(kc + 1) * 128], xn2[:, kc, pcols], kc == 0, kc == KC - 1) for kc in range(KC)],
                                         reads=[wgr.b, xn2B[(c0 + pp_ * TG) // G]], writes=[bank.b])
                                    S.op("act", ACT(GX.t[:, lc], bank.t[:], AF.Identity, bias=ppc("bin", l * 72 + 46 + c)), reads=[bank.b, ppt.b], writes=[GX.b])
                            jst[k] = (A, U, GX, Zb, trs, Qb)

                        def stage1b(k):
                            dr, b_, sp = jobs[k]
                            c0 = b_ * SL + sp * SPW
                            gsp = b_ * NSP + sp
                            A, U, GX, Zb, trs, Qb = jst[k]
                            pidx = l * 20 + dr * 10 + c
                            for pp_ in range(PPS):
                                cols = slice(c0 + pp_ * TG, c0 + (pp_ + 1) * TG)
                                lc = slice(pp_ * TG, (pp_ + 1) * TG)
                                Tr, Ti = trs[pp_]
                                S.op("act", ACT(A.t[:, lc], Tr.t[:], AF.Exp, bias=dpc("lam", pidx), scale=dpc("lam", pidx)), reads=[Tr.b, dpt.b], writes=[A.b])
                                S.op("dve", STT(U.t[:, lc], Ti.t[:], 1.0, xc[:, cols], ALU.add, ALU.mult), reads=[Ti.b, xcB[gsp]], writes=[U.b])
                            S.op("pool", TT(Qb.t[:], A.t[:], A.t[:], ALU.mult), reads=[A.b], writes=[Qb.b])
                            if dr == 1:
                                S.op("pool", TT(Zb.t[:], GX.t[:], GX.t[:], ALU.mult), reads=[GX.b], writes=[Zb.b])
                                S.op("pool", TS(Zb.t[:], Zb.t[:], 0.044715, 1.0, ALU.mult, ALU.add), reads=[Zb.b], writes=[Zb.b])
                                S.op("pool", TT(Zb.t[:], Zb.t[:], GX.t[:], ALU.mult), reads=[Zb.b, GX.b], writes=[Zb.b])

                        def stage2a(k):
                            dr, b_, sp = jobs[k]
                            A, U, GX, Zb, trs, Qb = jst[k]
                            S.op("act", ACT(Qb.t[:], Qb.t[:], AF.Sqrt, bias=qb.t[:, 0:1], scale=-0.0625), reads=[Qb.b, qb.b], writes=[Qb.b])

                        def stage2a_gelu(k):
                            dr, b_, sp = jobs[k]
                            A, U, GX, Zb, trs, Qb = jst[k]
                            if dr == 1:
                                S.op("act", ACT(Zb.t[:], Zb.t[:], AF.Tanh, scale=GELU_C), reads=[Zb.b], writes=[Zb.b])

                        def stage2b(k):
                            dr, b_, sp = jobs[k]
                            A, U, GX, Zb, trs, Qb = jst.pop(k)
                            c0 = b_ * SL + sp * SPW
                            cols = slice(c0, c0 + SPW)
                            S.op("dve", TT(U.t[:], Qb.t[:], U.t[:], ALU.mult), reads=[Qb.b, U.b], writes=[U.b])
                            if dr == 0:
                                init = 0.0 if sp == 0 else hf.t[:, c0 - 1:c0]
                                S.op("dve", lambda h: h.tensor_tensor_scan(
                                    out=hf.t[:, cols], data0=A.t[:], data1=U.t[:], initial=init, op0=ALU.mult, op1=ALU.add),
                                    reads=[A.b, U.b, hf.b], writes=[hf.b])
                            else:
                                first = (sp == NSP - 1)
                                hi_prev = hini[(sp + 1) % 2]
                                hi_cur = hini[sp % 2]
                                init = 0.0 if first else hi_prev.t[:, 0:1]
                                rd = [A.b, U.b] + ([] if first else [hi_prev.b])
                                S.op("dve", lambda h: h.tensor_tensor_scan(
                                    out=HB.t[:, ::-1], data0=A.t[:, ::-1], data1=U.t[:, ::-1], initial=init, op0=ALU.mult, op1=ALU.add),
                                    reads=rd, writes=[HB.b])
                                S.op("dve", CP(hi_cur.t[:, 0:1], HB.t[:, 0:1]), reads=[HB.b], writes=[hi_cur.b])
                                S.op("dve", STT(Zb.t[:], Zb.t[:], 1.0, GX.t[:], ALU.add, ALU.mult), reads=[Zb.b, GX.b], writes=[Zb.b])
                                S.op("dve", TT(HS.t[:], hf.t[:, cols], HB.t[:], ALU.add), reads=[hf.b, HB.b], writes=[HS.b])
                                yp = YB[cnt["yb"] % 2]
                                cnt["yb"] += 1
                                S.op("dve", TT(yp.t[:], HS.t[:], Zb.t[:], ALU.mult), reads=[HS.b, Zb.b], writes=[yp.b])
                                S.dma("sp", yb_d[:, c, cols], yp.t[:], reads=[yp.b], writes=[ybB[c]])

                        for k in range(len(jobs) + 1):
                            pend_conv = []
                            if k < len(jobs):
                                stage1a(k)
                                if k == 0 and NF > 1:
                                    pend_conv.append((c, jobs[1][1], jobs[1][2]))
                                if k + 2 < len(jobs) and jobs[k + 2][0] == 0:
                                    pend_conv.append((c, jobs[k + 2][1], jobs[k + 2][2]))
                                for pc_ in pend_conv:
                                    conv_mm(*pc_)
                            if k >= 1:
                                stage2a(k - 1)
                            if k < len(jobs):
                                stage1b(k)
                            if k >= 1:
                                stage2a_gelu(k - 1)
                            if k < len(jobs):
                                for pc_ in pend_conv:
                                    conv_ev(*pc_)
                                if k == NF and c + 1 < 10:
                                    xr_proj(c + 1, Wc[c + 1][0])
                                    conv_diag(c + 1)
                                if k == len(jobs) - 1 and c + 1 < 10:
                                    conv_sp(c + 1, 0, 0)
                            if k >= 1:
                                stage2b(k - 1)
                    S.barrier()

                with ExitStack() as stk:
                    alloc_wbufs(stk, 2048)
                    st = alloc_ac(stk)
                    yab_ = TB(sb(stk, "yag", [128, 4, G], BF16))
                    ybb_ = TB(sb(stk, "ybg", [128, 10, G], BF16))
                    mg = TB(sb(stk, "mg", [128, KC, G], BF16))
                    mgB = [Buf() for _ in range(KC)]
                    Tga = [TB(sb(stk, "Tga", [128, TG], F32)) for _ in range(2)]
                    Tgb = [TB(sb(stk, "Tgb", [128, TG], F32)) for _ in range(2)]
                    m1 = [TB(sb(stk, "m1", [128, TG], F32)) for _ in range(2)]
                    m2 = [TB(sb(stk, "m2", [128, TG], F32)) for _ in range(2)]
                    ci = 0
                    def c_loads(gj):
                        gsl_ = slice(gj * G, (gj + 1) * G)
                        S.dma("sp", yab_.t[:], ya_d[:, :, gsl_], reads=yaB, writes=[yab_.b])
                        S.dma("sp", ybb_.t[:], yb_d[:, :, gsl_], reads=ybB, writes=[ybb_.b])
                        S.dma("sp", st["xgs"][gj % 2][:], xres_d[:, :, gsl_], reads=[xresB[gj]], writes=st["xgBs"][gj % 2])

                    c_loads(0)
                    for gi in range(NG):
                        xg, xgB = st["xgs"][gi % 2], st["xgBs"][gi % 2]
                        gsl = slice(gi * G, (gi + 1) * G)
                        for m in range(8):
                            wpa_ = load_w(wb["wpa"][l, m], 512, ("wpa", l, m))
                            wpb_ = load_w(wb["wpb"][l, m], 1280, ("wpb", l, m))
                            wga_ = load_w(wb["win"][l, 56 + m], 1024, ("win", l, 56 + m))
                            wgb_ = load_w(wb["win"][l, 64 + m], 1024, ("win", l, 64 + m))
                            for t in range(G // TG):
                                cols = slice(t * TG, (t + 1) * TG)
                                gcols = slice(gi * G + t * TG, gi * G + (t + 1) * TG)
                                b_pa, b_pb, b_ga, b_gb = S.bank(), S.bank(), S.bank(), S.bank()
                                S.mm([MM(b_pa.t[:], wpa_.t[:, kc * 128:(kc + 1) * 128], yab_.t[:, kc, cols], kc == 0, kc == 3) for kc in range(4)],
                                     reads=[wpa_.b, yab_.b], writes=[b_pa.b])
                                S.mm([MM(b_pb.t[:], wpb_.t[:, kc * 128:(kc + 1) * 128], ybb_.t[:, kc, cols], kc == 0, kc == 9) for kc in range(10)],
                                     reads=[wpb_.b, ybb_.b], writes=[b_pb.b])
                                S.mm([MM(b_ga.t[:], wga_.t[:, kc * 128:(kc + 1) * 128], xn2[:, kc, gcols], kc == 0, kc == KC - 1) for kc in range(KC)],
                                     reads=[wga_.b, xn2B[gi]], writes=[b_ga.b])
                                S.mm([MM(b_gb.t[:], wgb_.t[:, kc * 128:(kc + 1) * 128], xn2[:, kc, gcols], kc == 0, kc == KC - 1) for kc in range(KC)],
                                     reads=[wgb_.b, xn2B[gi]], writes=[b_gb.b])
                                ta, tb_, a1, a2_ = Tga[ci % 2], Tgb[ci % 2], m1[ci % 2], m2[ci % 2]
                                ci += 1
                                S.op("act", ACT(ta.t[:], b_ga.t[:], AF.Tanh, bias=dpc("bin", l * 72 + 56 + m), scale=0.5), reads=[b_ga.b, dpt.b], writes=[ta.b])
                                S.op("act", ACT(tb_.t[:], b_gb.t[:], AF.Tanh, bias=dpc("bin", l * 72 + 64 + m), scale=0.5), reads=[b_gb.b, dpt.b], writes=[tb_.b])
                                S.op("dve", STT(a1.t[:], ta.t[:], 1.0, b_pa.t[:], ALU.add, ALU.mult), reads=[ta.b, b_pa.b], writes=[a1.b])
                                S.op("dve", STT(a2_.t[:], tb_.t[:], 1.0, b_pb.t[:], ALU.add, ALU.mult), reads=[tb_.b, b_pb.b], writes=[a2_.b])
                                S.op("dve", TT(mg.t[:, m, cols], a1.t[:], a2_.t[:], ALU.add), reads=[a1.b, a2_.b], writes=[mgB[m]])
                        pend = None
                        for mo in range(8):
                            wo_ = load_w(wb["wo"][l, mo], 1024, ("wo", l, mo))
                            for t in range(G // TG):
                                cols = slice(t * TG, (t + 1) * TG)
                                by = S.bank()
                                S.mm([MM(by.t[:], wo_.t[:, kc * 128:(kc + 1) * 128], mg.t[:, kc, cols], kc == 0, kc == KC - 1) for kc in range(KC)],
                                     reads=[wo_.b] + mgB, writes=[by.b])
                                S.op("dve", STT(xg[:, mo, cols], by.t[:], 0.5, xg[:, mo, cols], ALU.mult, ALU.add), reads=[by.b, xgB[mo]], writes=[xgB[mo]])
                                nsq = norm_sq(xg, xgB, mo, cols, st)
                                if pend is not None:
                                    norm_mm(*pend)
                                pend = nsq
                        norm_mm(*pend)
                        if gi + 1 < NG:
                            c_loads(gi + 1)
                        ffn(l, 1, xg, xgB, G, st, True)
                        if l + 1 < L:
                            phase_a(l + 1, gi, st, True)
                        else:
                            for t in range(G // TG):
                                cols = slice(t * TG, (t + 1) * TG)
                                norm_tile(xg, xgB, cols, "nf", 0, lambda kc: xg[:, kc, cols], lambda kc: xgB[kc], st, True)
                            for tt in range(G // 128):
                                io = st["io"][st["ioi"] % 2]
                                st["ioi"] += 1
                                for half in range(2):
                                    bank = S.bank()
                                    S.mm([TR(bank.t[:, j * 128:(j + 1) * 128], xg[:, half * 4 + j, tt * 128:(tt + 1) * 128], identf.t[:])
                                          for j in range(4)], reads=xgB + [identf.b], writes=[bank.b])
                                    S.op("act", ACT(io.t[:, half * 512:(half + 1) * 512], bank.t[:], AF.Copy), reads=[bank.b], writes=[io.b])
                                r0 = tok0 + gi * G + tt * 128
                                S.dma("pool", y_d[r0:r0 + 128, :], io.t[:], reads=[io.b], writes=[Buf()])
                    S.barrier()
        S.barrier()
    return nc


def _blk(w, kcn):
    K, N = w.shape
    return np.ascontiguousarray(w.reshape(kcn, 128, N // 128, 128).transpose(2, 1, 0, 3)).reshape(N // 128, 128, kcn * 128)


def prep_weights(inp, L):
    f32 = np.float32
    wgu = np.empty((L, 2, NFF, 128, 2 * KC * 128), f32)
    wd = np.empty((L, 2, 8, 2, 128, 11 * 128), f32)
    win = np.empty((L, 72, 128, KC * 128), f32)
    wrg = np.empty((L, 10, 128, 4 * 128), f32)
    wpa = np.empty((L, 8, 128, 4 * 128), f32)
    wpb = np.empty((L, 8, 128, 10 * 128), f32)
    wo = np.empty((L, 8, 128, 8 * 128), f32)
    for l in range(L):
        for f, pre in enumerate(("ffn1", "ffn2")):
            wgu[l, f, :, :, 0:1024] = _blk(np.asarray(inp[pre + "_w_gate"][l]), KC)
            wgu[l, f, :, :, 1024:2048] = _blk(np.asarray(inp[pre + "_w_up"][l]), KC)
            dwn = np.asarray(inp[pre + "_w_down"][l])
            for half in range(2):
                wd[l, f, :, half] = _blk(dwn[half * 1408:(half + 1) * 1408], 11)
        win[l] = _blk(np.asarray(inp["w_in"][l]), KC)
        wa = np.asarray(inp["rg_w_a"][l])
        wx = np.asarray(inp["rg_w_x"][l])
        for dr in range(2):
            wrg[l, :, :, (2 * dr) * 128:(2 * dr + 1) * 128] = wa[dr]
            wrg[l, :, :, (2 * dr + 1) * 128:(2 * dr + 2) * 128] = wx[dr]
        wpa[l] = _blk(np.asarray(inp["w_proj_a"][l]), 4)
        wpb[l] = _blk(np.asarray(inp["w_proj_b"][l]), 10)
        wo[l] = _blk(np.asarray(inp["w_out"][l]), 8)
    PO, NPP = pp_layout(L)
    pp = np.zeros((128, NPP), f32)

    def put(name, arr):
        a = np.asarray(arr, f32)
        a = a.reshape(-1, a.shape[-1] // 128, 128)
        a = a.transpose(2, 0, 1).reshape(128, -1)
        pp[:, PO[name]:PO[name] + a.shape[1]] = a

    put("n1", np.asarray(inp["ffn1_norm"])[:L])
    put("nm", np.asarray(inp["mix_norm"])[:L])
    put("n2", np.asarray(inp["ffn2_norm"])[:L])
    put("nf", np.asarray(inp["final_norm"])[None, :])
    put("bin", np.asarray(inp["b_in"])[:L])
    put("cw", np.asarray(inp["conv_w"])[:L].reshape(L * 4, 1280))
    put("cb", np.asarray(inp["conv_b"])[:L])
    put("ba", np.asarray(inp["rg_b_a"])[:L].reshape(L * 2, 1280))
    put("bx", np.asarray(inp["rg_b_x"])[:L].reshape(L * 2, 1280))
    put("lam", np.asarray(inp["rg_lambda"])[:L].reshape(L * 2, 1280))
    return {"wgu": wgu, "wd": wd, "win": win, "wrg": wrg, "wpa": wpa, "wpb": wpb, "wo": wo, "pp": pp,
            "b_in": np.ascontiguousarray(np.asarray(inp["b_in"], f32)[:L])}


def kernel(**inputs):
    xp = np.asarray(inputs["x_prompt"], np.float32)
    xs = np.asarray(inputs["x_sample"], np.float32)
    L = DEPTH
    B1, S1, _ = xp.shape
    B2, S2, _ = xs.shape
    n1 = B1 // NCORES
    n2 = B2 // NCORES
    seq_lens = [S1] * n1 + [S2] * n2
    wts = prep_weights(inputs, L)
    nc = build(seq_lens, L)
    in_maps = []
    for c in range(NCORES):
        xc = np.concatenate([xp[c * n1:(c + 1) * n1].reshape(-1, D), xs[c * n2:(c + 1) * n2].reshape(-1, D)], axis=0)
        m = {"x": np.ascontiguousarray(xc)}
        m.update(wts)
        in_maps.append(m)
    res = run_bass_kernel_spmd(nc, in_maps, core_ids=list(range(NCORES)))
    yp = np.empty_like(xp)
    ys = np.empty_like(xs)
    for c in range(NCORES):
        y = res.results[c]["y"]
        yp[c * n1:(c + 1) * n1] = y[:n1 * S1].reshape(n1, S1, D)
        ys[c * n2:(c + 1) * n2] = y[n1 * S1:].reshape(n2, S2, D)
    return (yp, ys)
```
